# Optimizing a Trainium2 kernel written in Bass

```python
import math
import jax, jax.numpy as jnp
from jax import lax
import numpy as np

D_MODEL = 2048
BATCH = 4
SEQ = 2048
DEPTH = 4

N_MIXERS = 3
N_META = 16
EPS = 1e-6
MLP_HIDDEN = 4 * D_MODEL
GDN_HEAD_DIM = 128
GDN_QK_HEADS = D_MODEL // 128
GDN_V_HEADS = 2 * GDN_QK_HEADS
GDN_QK_DIM = GDN_QK_HEADS * GDN_HEAD_DIM
GDN_V_DIM = GDN_V_HEADS * GDN_HEAD_DIM
GDN_CONV = 4
GDN_CHUNK = 64
GDN_IN = 2 * GDN_QK_DIM + 2 * GDN_V_DIM + 2 * GDN_V_HEADS
MLA_HEADS = D_MODEL // 128
MLA_Q_LORA = 3 * D_MODEL // 8
MLA_KV_LORA = D_MODEL // 4
MLA_NOPE = 128
MLA_ROPE = 64
MLA_V = 128
MLA_QK = MLA_NOPE + MLA_ROPE
MLA_IN = MLA_Q_LORA + MLA_KV_LORA + MLA_ROPE
ATTN_BLOCK = 128
ROPE_THETA = 10000.0
SSM_D_INNER = 2 * D_MODEL
SSM_HEAD_DIM = 64
SSM_HEADS = SSM_D_INNER // SSM_HEAD_DIM
SSM_GROUPS = 8
SSM_HPG = SSM_HEADS // SSM_GROUPS
SSM_STATE = 128
SSM_CONV = 4
SSM_CHUNK = 128
SSM_CONV_DIM = SSM_D_INNER + 2 * SSM_GROUPS * SSM_STATE
SSM_IN = SSM_D_INNER + SSM_CONV_DIM + SSM_HEADS
N_GDN = (DEPTH + 2) // 3
N_MLA = (DEPTH + 1) // 3
N_SSM = DEPTH // 3

kernel_name = 'hybrid_gdn_mla_ssd_trunk'

F32 = jnp.float32


def rms_norm(x, gain):
    xf = x.astype(F32)
    y = xf * lax.rsqrt(jnp.mean(xf * xf, axis=-1, keepdims=True) + EPS)
    return (y * gain.astype(F32)).astype(x.dtype)


def l2_normalize(x):
    xf = x.astype(F32)
    return (xf * lax.rsqrt(jnp.sum(xf * xf, axis=-1, keepdims=True) + EPS)).astype(x.dtype)


def causal_depthwise_conv(x, w, bias=None):
    width, ch = w.shape
    y = lax.conv_general_dilated(x, w[:, None, :], window_strides=(1,), padding=[(width - 1, 0)],
                                 dimension_numbers=('NWC', 'WIO', 'NWC'), feature_group_count=ch)
    if bias is not None:
        y = y + bias
    return y


def pad_time(t, n):
    return jnp.pad(t, [(0, 0), (n, 0)] + [(0, 0)] * (t.ndim - 2))


def gated_delta_rule_chunked(q, k, v, g, beta):
    dtype = v.dtype
    b, T, H, Dk = q.shape
    Dv = v.shape[-1]
    C = GDN_CHUNK
    n = T // C

    def chunks(t):
        t = t.astype(F32).reshape((b, n, C, H) + t.shape[3:])
        return jnp.moveaxis(t, 3, 2)

    q, k, v, g, beta = chunks(q), chunks(k), chunks(v), chunks(g), chunks(beta)
    G = jnp.cumsum(g, axis=-1)
    causal = jnp.tril(jnp.ones((C, C), bool))
    strict = jnp.tril(jnp.ones((C, C), bool), -1)
    decay = jnp.exp(jnp.where(causal, G[..., :, None] - G[..., None, :], -jnp.inf))
    kb = k * beta[..., None]
    A = jnp.where(strict, jnp.einsum('bnhid,bnhjd->bnhij', kb, k) * decay, 0.0)
    M = A + jnp.eye(C, dtype=F32)
    u = lax.linalg.triangular_solve(M, v * beta[..., None], left_side=True, lower=True, unit_diagonal=True)
    w = lax.linalg.triangular_solve(M, kb * jnp.exp(G)[..., None], left_side=True, lower=True, unit_diagonal=True)
    qk = jnp.einsum('bnhid,bnhjd->bnhij', q, k) * decay
    g_last = G[..., -1]
    k_dec = k * jnp.exp(g_last[..., None] - G)[..., None]
    q_dec = q * jnp.exp(G)[..., None]

    def step(S, xs):
        qd, qkc, uc, wc, kd, gl = xs
        v_new = uc - jnp.einsum('bhcd,bhde->bhce', wc, S)
        o = jnp.einsum('bhcd,bhde->bhce', qd, S) + jnp.einsum('bhij,bhje->bhie', qkc, v_new)
        S = S * jnp.exp(gl)[..., None, None] + jnp.einsum('bhcd,bhce->bhde', kd, v_new)
        return S, o

    S0 = jnp.zeros((b, H, Dk, Dv), F32)
    xs = tuple(jnp.moveaxis(t, 1, 0) for t in (q_dec, qk, u, w, k_dec, g_last))
    _, o = lax.scan(step, S0, xs)
    o = jnp.moveaxis(jnp.moveaxis(o, 0, 1), 3, 2).reshape(b, T, H, Dv)
    return o.astype(dtype)


def ssd_chunked(x, dt, A, Bm, Cm):
    dtype = x.dtype
    b, T, G, R, P = x.shape
    C = SSM_CHUNK
    n = T // C
    x = x.astype(F32).reshape(b, n, C, G, R, P)
    dt = dt.astype(F32).reshape(b, n, C, G, R)
    Bm = Bm.astype(F32).reshape(b, n, C, G, -1)
    Cm = Cm.astype(F32).reshape(b, n, C, G, -1)
    a_cum = jnp.cumsum(jnp.moveaxis(dt * A.astype(F32), 2, -1), axis=-1)
    causal = jnp.tril(jnp.ones((C, C), bool))
    Lmat = jnp.exp(jnp.where(causal, a_cum[..., :, None] - a_cum[..., None, :], -jnp.inf))
    xdt = x * dt[..., None]
    CB = jnp.einsum('bclgn,bcsgn->bcgls', Cm, Bm)
    scores = CB[:, :, :, None] * Lmat
    y_diag = jnp.einsum('bcghls,bcsghp->bclghp', scores, xdt)
    decay_states = jnp.exp(a_cum[..., -1:] - a_cum)
    states = jnp.einsum('bclgn,bcghl,bclghp->bcghpn', Bm, decay_states, xdt)
    chunk_decay = jnp.exp(a_cum[..., -1])

    def step(S, xs):
        st, dec = xs
        return S * dec[..., None, None] + st, S

    S0 = jnp.zeros((b, G, R, P, Bm.shape[-1]), F32)
    _, prev = lax.scan(step, S0, (jnp.moveaxis(states, 1, 0), jnp.moveaxis(chunk_decay, 1, 0)))
    prev = jnp.moveaxis(prev, 0, 1)
    y_off = jnp.einsum('bclgn,bcghpn,bcghl->bclghp', Cm, prev, jnp.exp(a_cum))
    return (y_diag + y_off).reshape(b, T, G, R, P).astype(dtype)


def causal_block_attention(q, k, v, scale):
    b, L, H, _ = q.shape
    Dv = v.shape[-1]
    kpos = jnp.arange(L)

    def attend(qb, qpos):
        s = jnp.einsum('bqhd,bkhd->bhqk', qb, k).astype(F32) * scale
        s = jnp.where(kpos[None, :] <= qpos[:, None], s, -jnp.inf)
        p = jax.nn.softmax(s, axis=-1).astype(v.dtype)
        return jnp.einsum('bhqk,bkhd->bqhd', p, v)

    o_meta = attend(q[:, :N_META], jnp.arange(N_META))
    nb = (L - N_META) // ATTN_BLOCK
    qr = jnp.moveaxis(q[:, N_META:].reshape(b, nb, ATTN_BLOCK, H, -1), 1, 0)

    def blk(args):
        qb, j = args
        return attend(qb, N_META + j * ATTN_BLOCK + jnp.arange(ATTN_BLOCK))

    o_real = lax.map(blk, (qr, jnp.arange(nb)))
    o_real = jnp.moveaxis(o_real, 0, 1).reshape(b, L - N_META, H, Dv)
    return jnp.concatenate([o_meta, o_real], axis=1)


def rope_tables(L):
    inv = ROPE_THETA ** (-jnp.arange(0, MLA_ROPE, 2, dtype=F32) / MLA_ROPE)
    ang = jnp.arange(L, dtype=F32)[:, None] * inv[None, :]
    ang = jnp.concatenate([ang, ang], axis=-1)
    return jnp.cos(ang), jnp.sin(ang)


def apply_rope(t, cos, sin):
    half = t.shape[-1] // 2
    rot = jnp.concatenate([-t[..., half:], t[..., :half]], axis=-1)
    return (t * cos[None, :, None, :] + rot * sin[None, :, None, :]).astype(t.dtype)


def gdn_mixer(u, w_in, conv_w, a_log, dt_bias, norm_w, w_out):
    b, L, _ = u.shape
    s1 = 2 * GDN_QK_DIM + GDN_V_DIM
    s2 = s1 + GDN_V_DIM
    s3 = s2 + GDN_V_HEADS
    qkv, z, bb, aa = jnp.split(u @ w_in, [s1, s2, s3], axis=-1)
    qkv = jax.nn.silu(causal_depthwise_conv(qkv, conv_w))
    q, k, v = jnp.split(qkv, [GDN_QK_DIM, 2 * GDN_QK_DIM], axis=-1)
    rep = GDN_V_HEADS // GDN_QK_HEADS
    q = jnp.repeat(l2_normalize(q.reshape(b, L, GDN_QK_HEADS, GDN_HEAD_DIM)), rep, axis=2) * (GDN_HEAD_DIM ** -0.5)
    k = jnp.repeat(l2_normalize(k.reshape(b, L, GDN_QK_HEADS, GDN_HEAD_DIM)), rep, axis=2)
    v = v.reshape(b, L, GDN_V_HEADS, GDN_HEAD_DIM)
    beta = jax.nn.sigmoid(bb)
    g = -jnp.exp(a_log) * jax.nn.softplus(aa + dt_bias)
    pad = GDN_CHUNK - N_META
    o = gated_delta_rule_chunked(pad_time(q, pad), pad_time(k, pad), pad_time(v, pad),
                                 pad_time(g, pad), pad_time(beta, pad))[:, pad:]
    o = rms_norm(o, norm_w) * jax.nn.silu(z.reshape(b, L, GDN_V_HEADS, GDN_HEAD_DIM))
    return o.reshape(b, L, GDN_V_DIM) @ w_out


def mla_mixer(u, w_in, norm_q_lat, norm_kv_lat, w_uq, w_ukv, q_norm, k_norm, w_out, cos, sin):
    b, L, _ = u.shape
    c_q, c_kv, k_pe = jnp.split(u @ w_in, [MLA_Q_LORA, MLA_Q_LORA + MLA_KV_LORA], axis=-1)
    c_q = rms_norm(c_q, norm_q_lat)
    c_kv = rms_norm(c_kv, norm_kv_lat)
    q = (c_q @ w_uq).reshape(b, L, MLA_HEADS, MLA_QK)
    kv = (c_kv @ w_ukv).reshape(b, L, MLA_HEADS, MLA_NOPE + MLA_V)
    q_nope, q_pe = q[..., :MLA_NOPE], q[..., MLA_NOPE:]
    k_nope, v = kv[..., :MLA_NOPE], kv[..., MLA_NOPE:]
    q_nope = rms_norm(q_nope, q_norm[:MLA_NOPE])
    q_pe = apply_rope(rms_norm(q_pe, q_norm[MLA_NOPE:]), cos, sin)
    k_nope = rms_norm(k_nope, k_norm[:MLA_NOPE])
    k_pe = apply_rope(rms_norm(k_pe[:, :, None, :], k_norm[MLA_NOPE:]), cos, sin)
    q_full = jnp.concatenate([q_nope, q_pe], axis=-1)
    k_full = jnp.concatenate([k_nope, jnp.broadcast_to(k_pe, (b, L, MLA_HEADS, MLA_ROPE))], axis=-1)
    o = causal_block_attention(q_full, k_full, v, MLA_QK ** -0.5)
    return o.reshape(b, L, MLA_HEADS * MLA_V) @ w_out


def ssm_mixer(u, w_in, conv_w, conv_b, a_log, dt_bias, d_skip, norm_w, w_out):
    b, L, _ = u.shape
    z, xbc, dt_raw = jnp.split(u @ w_in, [SSM_D_INNER, SSM_D_INNER + SSM_CONV_DIM], axis=-1)
    xbc = jax.nn.silu(causal_depthwise_conv(xbc, conv_w, conv_b))
    xs, Bm, Cm = jnp.split(xbc, [SSM_D_INNER, SSM_D_INNER + SSM_GROUPS * SSM_STATE], axis=-1)
    xs = xs.reshape(b, L, SSM_GROUPS, SSM_HPG, SSM_HEAD_DIM)
    Bm = Bm.reshape(b, L, SSM_GROUPS, SSM_STATE)
    Cm = Cm.reshape(b, L, SSM_GROUPS, SSM_STATE)
    dt = jax.nn.softplus(dt_raw + dt_bias).reshape(b, L, SSM_GROUPS, SSM_HPG)
    A = -jnp.exp(a_log).reshape(SSM_GROUPS, SSM_HPG)
    pad = SSM_CHUNK - N_META
    y = ssd_chunked(pad_time(xs, pad), pad_time(dt, pad), A, pad_time(Bm, pad), pad_time(Cm, pad))[:, pad:]
    y = y + xs * d_skip.reshape(SSM_GROUPS, SSM_HPG)[..., None]
    y = y.reshape(b, L, SSM_D_INNER) * jax.nn.silu(z)
    y = rms_norm(y.reshape(b, L, SSM_GROUPS, -1), norm_w.reshape(SSM_GROUPS, -1)).reshape(b, L, SSM_D_INNER)
    return y @ w_out


def sq_relu_mlp(u, w_up, w_down):
    return jnp.square(jax.nn.relu(u @ w_up)) @ w_down


def setup_inputs(seed: int = 0) -> dict:
    key = jax.random.key(seed)
    ks = iter(jax.random.split(key, 48))

    def normal(shape, scale):
        return jax.random.normal(next(ks), shape, F32) * scale

    def gain(shape):
        return 1.0 + normal(shape, 0.02)

    def dt_bias(shape):
        dt = jnp.exp(jax.random.uniform(next(ks), shape, F32, math.log(1e-3), math.log(1e-1)))
        return dt + jnp.log(-jnp.expm1(-dt))

    def a_log(shape):
        return jnp.log(jax.random.uniform(next(ks), shape, F32, 1.0, 16.0))

    D = D_MODEL
    return {
        'x': normal((BATCH, SEQ, D), 1.0),
        'meta_tokens': normal((N_META, D), 1.0),
        'norm_mix': gain((DEPTH, D)),
        'norm_mlp': gain((DEPTH, D)),
        'mlp_w_up': normal((DEPTH, D, MLP_HIDDEN), D ** -0.5),
        'mlp_w_down': normal((DEPTH, MLP_HIDDEN, D), MLP_HIDDEN ** -0.5),
        'gdn_w_in': normal((N_GDN, D, GDN_IN), D ** -0.5),
        'gdn_conv_w': normal((N_GDN, GDN_CONV, 2 * GDN_QK_DIM + GDN_V_DIM), GDN_CONV ** -0.5),
        'gdn_a_log': a_log((N_GDN, GDN_V_HEADS)),
        'gdn_dt_bias': dt_bias((N_GDN, GDN_V_HEADS)),
        'gdn_norm': gain((N_GDN, GDN_HEAD_DIM)),
        'gdn_w_out': normal((N_GDN, GDN_V_DIM, D), GDN_V_DIM ** -0.5),
        'mla_w_in': normal((N_MLA, D, MLA_IN), D ** -0.5),
        'mla_norm_q_lat': gain((N_MLA, MLA_Q_LORA)),
        'mla_norm_kv_lat': gain((N_MLA, MLA_KV_LORA)),
        'mla_w_uq': normal((N_MLA, MLA_Q_LORA, MLA_HEADS * MLA_QK), MLA_Q_LORA ** -0.5),
        'mla_w_ukv': normal((N_MLA, MLA_KV_LORA, MLA_HEADS * (MLA_NOPE + MLA_V)), MLA_KV_LORA ** -0.5),
        'mla_q_norm': gain((N_MLA, MLA_QK)),
        'mla_k_norm': gain((N_MLA, MLA_QK)),
        'mla_w_out': normal((N_MLA, MLA_HEADS * MLA_V, D), (MLA_HEADS * MLA_V) ** -0.5),
        'ssm_w_in': normal((N_SSM, D, SSM_IN), D ** -0.5),
        'ssm_conv_w': normal((N_SSM, SSM_CONV, SSM_CONV_DIM), SSM_CONV ** -0.5),
        'ssm_conv_b': normal((N_SSM, SSM_CONV_DIM), 0.01),
        'ssm_a_log': a_log((N_SSM, SSM_HEADS)),
        'ssm_dt_bias': dt_bias((N_SSM, SSM_HEADS)),
        'ssm_d': gain((N_SSM, SSM_HEADS)),
        'ssm_norm': gain((N_SSM, SSM_D_INNER)),
        'ssm_w_out': normal((N_SSM, SSM_D_INNER, D), SSM_D_INNER ** -0.5),
    }


def reference(x, meta_tokens, norm_mix, norm_mlp, mlp_w_up, mlp_w_down,
              gdn_w_in, gdn_conv_w, gdn_a_log, gdn_dt_bias, gdn_norm, gdn_w_out,
              mla_w_in, mla_norm_q_lat, mla_norm_kv_lat, mla_w_uq, mla_w_ukv, mla_q_norm, mla_k_norm, mla_w_out,
              ssm_w_in, ssm_conv_w, ssm_conv_b, ssm_a_log, ssm_dt_bias, ssm_d, ssm_norm, ssm_w_out):
    b = x.shape[0]
    meta = jnp.broadcast_to(meta_tokens.astype(x.dtype)[None], (b, N_META, x.shape[-1]))
    h = jnp.concatenate([meta, x], axis=1)
    cos, sin = rope_tables(h.shape[1])
    ia = ib = ic = 0
    for i in range(DEPTH):
        u = rms_norm(h, norm_mix[i])
        kind = i % N_MIXERS
        if kind == 0:
            h = h + gdn_mixer(u, gdn_w_in[ia], gdn_conv_w[ia], gdn_a_log[ia], gdn_dt_bias[ia],
                              gdn_norm[ia], gdn_w_out[ia])
            ia += 1
        elif kind == 1:
            h = h + mla_mixer(u, mla_w_in[ib], mla_norm_q_lat[ib], mla_norm_kv_lat[ib], mla_w_uq[ib],
                              mla_w_ukv[ib], mla_q_norm[ib], mla_k_norm[ib], mla_w_out[ib], cos, sin)
            ib += 1
        else:
            h = h + ssm_mixer(u, ssm_w_in[ic], ssm_conv_w[ic], ssm_conv_b[ic], ssm_a_log[ic],
                              ssm_dt_bias[ic], ssm_d[ic], ssm_norm[ic], ssm_w_out[ic])
            ic += 1
        h = h + sq_relu_mlp(rms_norm(h, norm_mlp[i]), mlp_w_up[i], mlp_w_down[i])
    return h[:, N_META:]
```

```python
import numpy as np
from contextlib import ExitStack
import concourse.bass as bass
import concourse.mybir as mybir
from concourse.bass_utils import run_bass_kernel_spmd

ACT = mybir.ActivationFunctionType
ALU = mybir.AluOpType
AX = mybir.AxisListType
F32 = mybir.dt.float32
BF16 = mybir.dt.bfloat16

ENGS = ['pe', 'dve', 'act', 'pool', 'sp']
EPOCH = 7500
N_EPOCH_SEMS = 10
N_DMA_SEMS = 40

D = 2048
L = 2064
NMETA = 16
EPS = 1e-6
TT = [(0, 16), (16, 512), (528, 512), (1040, 512), (1552, 512)]
TK = [(0, 16)] + [(16 + 128 * i, 128) for i in range(16)]


class Op:
    __slots__ = ('i', 'eng', 'emit', 'dma', 'deps', 'sig', 'idx', 'slot', 'target', 'prev_target')

    def __init__(self, i, eng, emit, dma):
        self.i, self.eng, self.emit, self.dma = i, eng, emit, dma
        self.deps = ()
        self.sig = False
        self.idx = None
        self.slot = None
        self.target = None
        self.prev_target = 0


class Prog:
    def __init__(self, nc):
        self.nc = nc
        self.ops = []
        self.w = {}
        self.r = {}
        self.last = {}
        self.dma_since = []

    def add(self, eng, emit, reads=(), writes=(), acc=False, dma=False):
        op = Op(len(self.ops), eng, emit, dma)
        deps = {}
        for k in reads:
            for o in self.w.get(k, ()):
                deps[o.i] = o
        for k in writes:
            rs = self.r.get(k)
            ws = self.w.get(k)
            has_r = bool(rs) and (bool(rs[0]) or bool(rs[1]))
            if has_r:
                for o in rs[0].values():
                    deps[o.i] = o
                for o in rs[1]:
                    deps[o.i] = o
            if acc and not has_r and ws:
                ws.append(op)
            else:
                if ws:
                    for o in ws:
                        deps[o.i] = o
                self.w[k] = [op]
                self.r[k] = ({}, [])
        for k in reads:
            rs = self.r.get(k)
            if rs is None:
                rs = self.r[k] = ({}, [])
            if dma:
                rs[1].append(op)
            else:
                rs[0][eng] = op
        deps.pop(op.i, None)
        self._set_deps(op, deps.values())
        self.ops.append(op)
        if dma:
            self.dma_since.append(op)
        else:
            self.last[eng] = op
        return op

    def _set_deps(self, op, deps):
        red = {}
        dl = []
        for o in deps:
            if o.dma:
                dl.append(o)
            else:
                if o.eng == 'pe' and op.eng == 'pe' and not op.dma:
                    continue
                c = red.get(o.eng)
                if c is None or c.i < o.i:
                    red[o.eng] = o
        op.deps = list(red.values()) + dl

    def barrier(self):
        deps = list(self.last.values()) + list(self.dma_since)
        for e in ENGS:
            op = Op(len(self.ops), e, None, False)
            op.deps = [o for o in deps if not (not o.dma and o.eng == e)]
            self.ops.append(op)
        self.dma_since = []
        self.w = {}
        self.r = {}

    def emit_all(self, sems):
        for op in self.ops:
            for d in op.deps:
                d.sig = True
        cnt = {e: 0 for e in ENGS}
        ndma = 0
        slot_uses = [0] * N_DMA_SEMS
        for op in self.ops:
            if op.dma:
                s = ndma % N_DMA_SEMS
                ndma += 1
                op.slot = s
                op.prev_target = 16 * slot_uses[s]
                slot_uses[s] += 1
                op.target = 16 * slot_uses[s]
            elif op.sig:
                cnt[op.eng] += 1
                op.idx = cnt[op.eng]
        for e in ENGS:
            assert cnt[e] < EPOCH * N_EPOCH_SEMS, (e, cnt[e])
        self.stats = dict(cnt=cnt, ndma=ndma, nops=len(self.ops))
        by_eng = {e: [o for o in self.ops if o.eng == e] for e in ENGS}

        def run(eng_name, e):
            waited = {}
            dwaited = {}
            for op in by_eng[eng_name]:
                for d in op.deps:
                    if d.dma:
                        if dwaited.get(d.slot, 0) < d.target:
                            e.wait_ge(sems['dma'][d.slot], d.target)
                            dwaited[d.slot] = d.target
                    else:
                        ep, val = divmod(d.idx - 1, EPOCH)
                        val += 1
                        w = waited.get(d.eng)
                        if w is not None and (w[0] > ep or (w[0] == ep and w[1] >= val)):
                            continue
                        e.wait_ge(sems[d.eng][ep], val)
                        waited[d.eng] = (ep, val)
                if op.dma and op.prev_target > 0 and dwaited.get(op.slot, 0) < op.prev_target:
                    e.wait_ge(sems['dma'][op.slot], op.prev_target)
                    dwaited[op.slot] = op.prev_target
                if op.emit is None:
                    continue
                ins = op.emit(e)
                if op.dma:
                    ins.then_inc(sems['dma'][op.slot], 16)
                elif op.sig:
                    ins.then_inc(sems[op.eng][(op.idx - 1) // EPOCH], 1)

        return run


class Arena:
    def __init__(self, t, nwords):
        self.t, self.n, self.off = t, nwords, 0

    def f32(self, n):
        assert self.off + n <= self.n, (self.off, n, self.n)
        ap = self.t[:, self.off:self.off + n]
        self.off += n
        return ap

    def bf16(self, n):
        w = (n + 1) // 2
        return self.f32(w).bitcast(BF16)


class K:
    def __init__(self, nc, st, dr):
        self.nc, self.st, self.dr = nc, st, dr
        self.p = Prog(nc)
        self.psn = 0
        self.wsn = 0
        self.hbn = 0
        self.tmpn = 0
        self.debug = False
        self.dbg = []

    def ps_next(self, banks=None):
        banks = banks or list(range(8))
        b = banks[self.psn % len(banks)]
        self.psn += 1
        return self.ps[b], ('ps', b)


def alloc(k):
    nc, st = k.nc, k.st
    NW = 53100
    k.arena_t = st.enter_context(nc.sbuf_tensor("arena", [128, NW], F32))
    ar = Arena(k.arena_t, NW)
    k.regA = ar.f32(16512)
    k.regB = ar.f32(16512)
    k.wsl = [ar.bf16(8192) for _ in range(2)]
    k.hb = [ar.f32(L) for _ in range(2)]
    k.tmp = [ar.f32(512) for _ in range(3)]
    k.cv = ar.f32(max(b[1] for b in k.cvo['__blocks__'].values()))
    k.identf = ar.f32(128)
    k.identb = ar.bf16(128)
    k.onesb = ar.bf16(128)
    k.regC = ar.f32(5164)
    k.trib = ar.bf16(128)
    k.zb = ar.bf16(128)
    k.rmb = ar.bf16(64)
    k.ar = ar
    k.ps = [st.enter_context(nc.psum_tensor(f"ps{i}", [128, 512], F32))[:, :] for i in range(8)]
    k.uT = k.regA.bitcast(BF16).rearrange("p (c t) -> p c t", c=16)


def load_consts(k):
    p, dr = k.p, k.dr
    p.add('sp', lambda e: e.dma_start(out=k.identf, in_=dr['ident']), writes=['identf'], dma=True)
    p.add('dve', lambda e: e.tensor_copy(out=k.identb, in_=k.identf), reads=['identf'], writes=['identb'])
    p.add('dve', lambda e: e.memset(k.onesb, 1.0), writes=['onesb'])
    p.add('dve', lambda e: e.memset(k.zb, 0.0), writes=['zb'])
    p.add('sp', lambda e: e.dma_start(out=k.tmp[0][:, :128], in_=dr['tri']), writes=[('tmp', 0)], dma=True)
    p.add('dve', lambda e: e.tensor_copy(out=k.trib, in_=k.tmp[0][:, :128]), reads=[('tmp', 0)], writes=['trib'])
    p.add('sp', lambda e: e.dma_start(out=k.tmp[1][:64, :64], in_=dr['rm']), writes=[('tmp', 1)], dma=True)
    p.add('dve', lambda e: e.tensor_copy(out=k.rmb[:64, :], in_=k.tmp[1][:64, :64]), reads=[('tmp', 1)], writes=['rmb'])


def cvs(k, name, i=0, n=None):
    off, w = k.cvo[name]
    for (b0, bs) in k.cvo['__blocks__'].values():
        if b0 <= off < b0 + bs:
            off -= b0
            break
    n = w if n is None else n
    return k.cv[:, off + i:off + i + n]


def dump(k, name, ap, key, dt=F32):
    if not getattr(k, 'debug', False):
        return
    t = k.nc.dram_tensor('dbg_' + name, list(ap.shape), dt, kind="ExternalOutput").ap()
    k.p.add('sp', lambda e: e.dma_start(out=t, in_=ap), reads=[key], writes=[('dbg', name)], dma=True)
    k.dbg.append(('dbg', name))


def wload(k, w_ap, K, c0, cn):
    KC = K // 128
    assert KC * cn <= 8192
    s = k.wsn % 2
    k.wsn += 1
    view = k.wsl[s][:, :KC * cn].rearrange("p (kc f) -> p kc f", kc=KC)
    wv = w_ap.rearrange("(kc p) f -> p kc f", p=128)
    g = 8
    for k0 in range(0, KC, g):
        k1 = min(KC, k0 + g)
        k.p.add('pool', lambda e, k0=k0, k1=k1: e.dma_start(out=view[:, k0:k1, :], in_=wv[:, k0:k1, c0:c0 + cn]),
                writes=[('wsl', s)], acc=(k0 > 0), dma=True)
    return view, ('wsl', s)


def linear(k, w_ap, K, f_list, src, consume, tiles=TT, banks=None):
    KC = K // 128
    fw = 8192 // KC
    i = 0
    while i < len(f_list):
        j = i
        c0 = f_list[i][0]
        while j + 1 < len(f_list) and f_list[j + 1][0] + f_list[j + 1][1] - c0 <= fw:
            j += 1
        c1 = f_list[j][0] + f_list[j][1]
        view, wkey = wload(k, w_ap, K, c0, c1 - c0)
        for (f0, fn) in f_list[i:j + 1]:
            for (t0, tn) in tiles:
                ps, pk = k.ps_next(banks)
                for kc in range(KC):
                    s_ap, skey = src(kc, t0, tn)
                    k.p.add('pe', lambda e, ps=ps, kc=kc, f0=f0, fn=fn, tn=tn, s_ap=s_ap, view=view, c0=c0: e.matmul(
                        ps[:fn, :tn], lhsT=view[:, kc, f0 - c0:f0 - c0 + fn], rhs=s_ap,
                        start=(kc == 0), stop=(kc == KC - 1)),
                        reads=[wkey, skey], writes=[pk], acc=(kc > 0))
                consume(f0, fn, t0, tn, ps[:fn, :tn], pk)
        i = j + 1


def linear_tm(k, w_ap, K, c_list, src, consume, tiles=TK, banks=None, slot=None):
    KC = K // 128
    fw = 8192 // KC
    i = 0
    while i < len(c_list):
        j = i
        c0 = c_list[i][0]
        while j + 1 < len(c_list) and c_list[j + 1][0] + c_list[j + 1][1] - c0 <= fw:
            j += 1
        c1 = c_list[j][0] + c_list[j][1]
        if slot is not None:
            c0, c1 = slot
        view, wkey = wload(k, w_ap, K, c0, c1 - c0)
        for (f0, fn) in c_list[i:j + 1]:
            for (t0, tn) in tiles:
                ps, pk = k.ps_next(banks)
                for kc in range(KC):
                    s_ap, skey = src(kc, t0, tn)
                    k.p.add('pe', lambda e, ps=ps, kc=kc, f0=f0, fn=fn, tn=tn, s_ap=s_ap, view=view, c0=c0: e.matmul(
                        ps[:tn, :fn], lhsT=s_ap, rhs=view[:, kc, f0 - c0:f0 - c0 + fn],
                        start=(kc == 0), stop=(kc == KC - 1)),
                        reads=[wkey, skey], writes=[pk], acc=(kc > 0))
                consume(f0, fn, t0, tn, ps[:tn, :fn], pk)
        i = j + 1


def src_uT(k):
    return lambda kc, t0, tn: (k.uT[:, kc, t0:t0 + tn], ('uT', kc))


def proj_residual(k, w_ap, K, src, banks=None):
    p = k.p
    hT = k.dr['hT']
    state = {}

    def consume(f0, fn, t0, tn, ps, pk):
        c = f0 // 128
        if t0 == 0:
            s = k.hbn % 2
            k.hbn += 1
            state['s'] = s
            p.add('sp', lambda e, c=c, s=s: e.dma_start(out=k.hb[s], in_=hT[c]), reads=[('hT', c)], writes=[('hb', s)], dma=True)
        s = state['s']
        p.add('dve', lambda e, s=s, t0=t0, tn=tn, ps=ps: e.tensor_tensor(out=k.hb[s][:, t0:t0 + tn], in0=k.hb[s][:, t0:t0 + tn], in1=ps, op=ALU.add),
              reads=[pk, ('hb', s)], writes=[('hb', s)])
        if t0 == TT[-1][0]:
            p.add('sp', lambda e, c=c, s=s: e.dma_start(out=hT[c], in_=k.hb[s]), reads=[('hb', s)], writes=[('hT', c)], dma=True)

    linear(k, w_ap, K, [(c * 128, 128) for c in range(16)], src, consume, banks=banks)


def rmsnorm_hT(k, gname):
    p = k.p
    hT = k.dr['hT']
    sq = [k.regB[:, i * 1032:(i + 1) * 1032].bitcast(BF16) for i in range(2)]
    rstd = k.regB[:, 2064:2064 + L]
    t1 = k.regB[:, 4200:4200 + L]
    banks = [k.ps[b] for b in range(5)]
    for c in range(16):
        s = k.hbn % 2
        k.hbn += 1
        p.add('sp', lambda e, c=c, s=s: e.dma_start(out=k.hb[s], in_=hT[c]), reads=[('hT', c)], writes=[('hb', s)], dma=True)
        q = c % 2
        p.add('act', lambda e, s=s, q=q: e.activation(out=sq[q], in_=k.hb[s], func=ACT.Square), reads=[('hb', s)], writes=[('sq', q)])
        for ti, (t0, tn) in enumerate(TT):
            p.add('pe', lambda e, ti=ti, q=q, t0=t0, tn=tn, c=c: e.matmul(banks[ti][:, :tn], lhsT=k.onesb, rhs=sq[q][:, t0:t0 + tn],
                                                                         start=(c == 0), stop=(c == 15)),
                  reads=[('sq', q), 'onesb'], writes=[('ps', ti)], acc=(c > 0))
    for ti, (t0, tn) in enumerate(TT):
        p.add('dve', lambda e, ti=ti, t0=t0, tn=tn: e.tensor_scalar(out=t1[:, t0:t0 + tn], in0=banks[ti][:, :tn], scalar1=1.0 / D, scalar2=EPS,
                                                                   op0=ALU.mult, op1=ALU.add), reads=[('ps', ti)], writes=[('t1', ti)])
        p.add('act', lambda e, t0=t0, tn=tn: e.activation(out=t1[:, t0:t0 + tn], in_=t1[:, t0:t0 + tn], func=ACT.Sqrt), reads=[('t1', ti)], writes=[('t1', ti)])
        p.add('dve', lambda e, t0=t0, tn=tn: e.reciprocal(out=rstd[:, t0:t0 + tn], in_=t1[:, t0:t0 + tn]), reads=[('t1', ti)], writes=['rstd'], acc=(ti > 0))
    for c in range(16):
        s = k.hbn % 2
        k.hbn += 1
        p.add('sp', lambda e, c=c, s=s: e.dma_start(out=k.hb[s], in_=hT[c]), reads=[('hT', c)], writes=[('hb', s)], dma=True)
        p.add('dve', lambda e, c=c, s=s: e.scalar_tensor_tensor(out=k.uT[:, c, :], in0=k.hb[s], scalar=cvs(k, gname, c, 1), in1=rstd,
                                                               op0=ALU.mult, op1=ALU.mult),
              reads=[('hb', s), 'rstd', 'cv'], writes=[('uT', c)])


def mlp(k, l):
    p = k.p
    rmsnorm_hT(k, f'norm_mlp{l}')
    k.p.barrier()
    hid = k.regB.bitcast(BF16).rearrange("p (c t) -> p c t", c=16)
    wup = k.dr['mlp_w_up'][l]
    wdn = k.dr['mlp_w_down'][l]
    for g in range(4):
        def consume(f0, fn, t0, tn, ps, pk):
            j = f0 // 128
            i = k.tmpn % 3
            k.tmpn += 1
            p.add('act', lambda e, i=i, tn=tn, ps=ps: e.activation(out=k.tmp[i][:, :tn], in_=ps, func=ACT.Relu), reads=[pk], writes=[('tmp', i)])
            p.add('dve', lambda e, i=i, j=j, t0=t0, tn=tn: e.tensor_tensor(out=hid[:, j, t0:t0 + tn], in0=k.tmp[i][:, :tn], in1=k.tmp[i][:, :tn], op=ALU.mult),
                  reads=[('tmp', i)], writes=[('hid', j)], acc=(t0 > 0))

        linear(k, wup[:, g * 2048:(g + 1) * 2048], 2048, [(c * 128, 128) for c in range(16)], src_uT(k), consume)
        proj_residual(k, wdn[g * 2048:(g + 1) * 2048, :], 2048, lambda kc, t0, tn: (hid[:, kc, t0:t0 + tn], ('hid', kc)))


def load_input(k):
    p, dr = k.p, k.dr
    stage = k.regB[:, :4 * L].rearrange("p (c t) -> p c t", c=4)
    xin = [k.regA[:, i * 512:(i + 1) * 512] for i in range(3)]
    n = 0
    for cg in range(4):
        for ti, (t0, tn) in enumerate(TK):
            i = n % 3
            n += 1
            src = dr['meta'][:, cg * 512:(cg + 1) * 512] if ti == 0 else dr['x'][t0 - 16:t0 - 16 + tn, cg * 512:(cg + 1) * 512]
            p.add('sp', lambda e, i=i, tn=tn, src=src: e.dma_start(out=xin[i][:tn, :], in_=src), writes=[('xin', i)], dma=True)
            ps, pk = k.ps_next()
            for c in range(4):
                p.add('pe', lambda e, ps=ps, c=c, i=i, tn=tn: e.transpose(ps[:, c * 128:c * 128 + tn], xin[i][:tn, c * 128:(c + 1) * 128], k.identf[:tn, :tn]),
                      reads=[('xin', i), 'identf'], writes=[pk], acc=(c > 0))
            p.add('act', lambda e, ps=ps, t0=t0, tn=tn: e.activation(out=stage[:, :, t0:t0 + tn], in_=ps.rearrange("p (c t) -> p c t", c=4)[:, :, :tn], func=ACT.Copy),
                  reads=[pk], writes=['stage'], acc=(ti > 0))
        for c in range(4):
            p.add('sp', lambda e, c=c, cg=cg: e.dma_start(out=dr['hT'][cg * 4 + c], in_=stage[:, c, :]), reads=['stage'], writes=[('hT', cg * 4 + c)], dma=True)


def store_output(k):
    p, dr = k.p, k.dr
    stage = k.regB[:, :4 * L].rearrange("p (c t) -> p c t", c=4)
    ob = [k.regA[:, i * 512:(i + 1) * 512] for i in range(3)]
    n = 0
    outs = []
    for cg in range(4):
        for c in range(4):
            p.add('sp', lambda e, c=c, cg=cg: e.dma_start(out=stage[:, c, :], in_=dr['hT'][cg * 4 + c]), reads=[('hT', cg * 4 + c)], writes=['stage'], acc=(c > 0), dma=True)
        for ti, (t0, tn) in enumerate(TK[1:]):
            i = n % 3
            n += 1
            ps, pk = k.ps_next()
            for c in range(4):
                p.add('pe', lambda e, ps=ps, c=c, t0=t0: e.transpose(ps[:, c * 128:(c + 1) * 128], stage[:, c, t0:t0 + 128], k.identf),
                      reads=['stage', 'identf'], writes=[pk], acc=(c > 0))
            p.add('act', lambda e, ps=ps, i=i: e.activation(out=ob[i], in_=ps, func=ACT.Copy), reads=[pk], writes=[('ob', i)])
            key = ('out', cg, ti)
            outs.append(key)
            p.add('sp', lambda e, i=i, t0=t0, cg=cg: e.dma_start(out=dr['out'][t0 - 16:t0 - 16 + 128, cg * 512:(cg + 1) * 512], in_=ob[i]),
                  reads=[('ob', i)], writes=[key], dma=True)
    p.add('sp', None, reads=outs + k.dbg)


def psnorm(k, ps, pk, rows, tn, gain, out_ap, out_key, out_acc, banks):
    p = k.p
    i = k.tmpn % 3
    k.tmpn += 1
    sqt = k.tmp[i].bitcast(BF16)[:rows, :tn]
    rs = k.tmp2[i][:rows, :tn]
    p.add('act', lambda e: e.activation(out=sqt, in_=ps, func=ACT.Square), reads=[pk], writes=[('tmp', i)])
    ps2, pk2 = k.ps_next(banks)
    p.add('pe', lambda e: e.matmul(ps2[:rows, :tn], lhsT=k.onesb[:rows, :rows], rhs=sqt, start=True, stop=True),
          reads=[('tmp', i), 'onesb'], writes=[pk2])
    p.add('dve', lambda e: e.tensor_scalar(out=rs, in0=ps2[:rows, :tn], scalar1=1.0 / rows, scalar2=EPS, op0=ALU.mult, op1=ALU.add),
          reads=[pk2], writes=[('tmp2', i)])
    p.add('act', lambda e: e.activation(out=rs, in_=rs, func=ACT.Sqrt), reads=[('tmp2', i)], writes=[('tmp2', i)])
    p.add('dve', lambda e: e.reciprocal(out=rs, in_=rs), reads=[('tmp2', i)], writes=[('tmp2', i)])
    p.add('dve', lambda e: e.scalar_tensor_tensor(out=out_ap, in0=ps, scalar=gain, in1=rs, op0=ALU.mult, op1=ALU.mult),
          reads=[pk, ('tmp2', i), 'cv'], writes=[out_key], acc=out_acc)


def rope(k, t_ap, t_key, t0, tn, out_ap, out_key, out_acc, banks):
    p = k.p
    i = k.tmpn % 3
    k.tmpn += 1
    a = k.tmp[i][:64, :tn]
    b = k.tmp2[i][:64, :tn]
    ps3, pk3 = k.ps_next(banks)
    p.add('pe', lambda e: e.matmul(ps3[:64, :tn], lhsT=k.rmb[:64, :64], rhs=t_ap, start=True, stop=True), reads=[t_key, 'rmb'], writes=[pk3])
    p.add('dve', lambda e: e.tensor_tensor(out=a, in0=ps3[:64, :tn], in1=k.sinT[:64, t0:t0 + tn], op=ALU.mult), reads=[pk3, 'cs'], writes=[('tmp', i)])
    p.add('pool', lambda e: e.tensor_tensor(out=b, in0=t_ap, in1=k.cosT[:64, t0:t0 + tn], op=ALU.mult), reads=[t_key, 'cs'], writes=[('tmp2', i)])
    p.add('dve', lambda e: e.tensor_tensor(out=out_ap, in0=a, in1=b, op=ALU.add), reads=[('tmp', i), ('tmp2', i)], writes=[out_key], acc=out_acc)


def mla(k, ib, l):
    p, dr = k.p, k.dr
    rmsnorm_hT(k, f'norm_mix{l}')
    p.barrier()
    NH = 16
    B = k.regB
    cqT = B[:, 0:6192].bitcast(BF16).rearrange("p (c t) -> p c t", c=6)
    ckvT = B[:, 6192:10320].bitcast(BF16).rearrange("p (c t) -> p c t", c=4)
    vaug = B[:, 10320:11425].bitcast(BF16).rearrange("p (m d) -> p m d", m=17)
    otm = [B[:, 11500 + 64 * i:11564 + 64 * i].bitcast(BF16) for i in range(2)]
    rcp = B[:, 11700:11720]
    kpn = B[:, 11800:12832].bitcast(BF16)
    qpn = B[:, 12832:13864].bitcast(BF16)
    k.tmp2 = [B[:, 13864 + 512 * i:14376 + 512 * i] for i in range(3)]
    k.cosT = k.regC[:, 0:L]
    k.sinT = k.regC[:, L:2 * L]
    krT = k.regC[:, 2 * L:2 * L + 1032].bitcast(BF16)
    qnT = k.hb[0][:, 0:1032].bitcast(BF16)
    knT = k.hb[0][:, 1032:2064].bitcast(BF16)
    qrT = k.hb[1][:, 0:1032].bitcast(BF16)
    pTs = [k.hb[1][:, 1032:2064].bitcast(BF16), B[:, 15400:16432].bitcast(BF16)]
    oT = k.regA.bitcast(BF16).rearrange("p (c t) -> p c t", c=16)
    w_in, w_uq, w_ukv, w_out = dr['mla_w_in'][ib], dr['mla_w_uq'][ib], dr['mla_w_ukv'][ib], dr['mla_w_out'][ib]
    p.add('sp', lambda e: e.dma_start(out=k.cosT[:64, :], in_=dr['cosT']), writes=['cs'], dma=True)
    p.add('sp', lambda e: e.dma_start(out=k.sinT[:64, :], in_=dr['sinT']), writes=['cs'], acc=True, dma=True)
    p.add('dve', lambda e: e.memset(vaug[:, :, 128:130], 1.0), writes=['vaug'])
    accb = [0, 1, 2, 3, 4]
    rot = [5, 6, 7]

    def latent(cols, nchunk, dst, dkey, gname, rows=128):
        def consume(f0, fn, t0, tn, ps, pk):
            j = (f0 - cols) // 128
            ti = [t[0] for t in TT].index(t0)
            i = k.tmpn % 3
            k.tmpn += 1
            sqt = k.tmp[i].bitcast(BF16)[:fn, :tn]
            p.add('act', lambda e: e.activation(out=dst[:fn, j, t0:t0 + tn], in_=ps, func=ACT.Copy), reads=[pk], writes=[(dkey, j)], acc=(t0 > 0))
            p.add('act', lambda e: e.activation(out=sqt, in_=ps, func=ACT.Square), reads=[pk], writes=[('tmp', i)])
            p.add('pe', lambda e: e.matmul(k.ps[ti][:fn, :tn], lhsT=k.onesb[:fn, :fn], rhs=sqt, start=(j == 0), stop=(j == nchunk - 1)),
                  reads=[('tmp', i), 'onesb'], writes=[('ps', ti)], acc=(j > 0))
        n = nchunk * 128 if rows == 128 else rows
        linear(k, w_in, 2048, [(cols + j * 128, min(128, n - j * 128)) for j in range(nchunk)], src_uT(k), consume, banks=rot)
        rr = rows
        for ti, (t0, tn) in enumerate(TT):
            rs = k.tmp2[ti % 3][:rr, :tn]
            p.add('dve', lambda e, rs=rs, ti=ti, tn=tn: e.tensor_scalar(out=rs, in0=k.ps[ti][:rr, :tn], scalar1=1.0 / n, scalar2=EPS, op0=ALU.mult, op1=ALU.add),
                  reads=[('ps', ti)], writes=[('tmp2', ti % 3)])
            p.add('act', lambda e, rs=rs: e.activation(out=rs, in_=rs, func=ACT.Sqrt), reads=[('tmp2', ti % 3)], writes=[('tmp2', ti % 3)])
            p.add('dve', lambda e, rs=rs: e.reciprocal(out=rs, in_=rs), reads=[('tmp2', ti % 3)], writes=[('tmp2', ti % 3)])
            for j in range(nchunk):
                p.add('dve', lambda e, rs=rs, j=j, t0=t0, tn=tn: e.scalar_tensor_tensor(out=dst[:rr, j, t0:t0 + tn], in0=dst[:rr, j, t0:t0 + tn],
                                                                                     scalar=cvs(k, gname, j, 1)[:rr, :], in1=rs, op0=ALU.mult, op1=ALU.mult),
                      reads=[(dkey, j), ('tmp2', ti % 3), 'cv'], writes=[(dkey, j)])

    latent(0, 6, cqT, 'cq', f'mla_nq{ib}')
    latent(768, 4, ckvT, 'ckv', f'mla_nkv{ib}')
    kpn3 = kpn.rearrange("p (c t) -> p c t", c=1)
    latent(1280, 1, kpn3, 'kpn', f'mla_kn_r{ib}', rows=64)
    for (t0, tn) in TT:
        rope(k, kpn[:64, t0:t0 + tn], ('kpn', 0), t0, tn, krT[:64, t0:t0 + tn], 'krT', t0 > 0, rot)
    p.barrier()

    def src_cq(kc, t0, tn):
        return cqT[:, kc, t0:t0 + tn], ('cq', kc)

    def src_ckv(kc, t0, tn):
        return ckvT[:, kc, t0:t0 + tn], ('ckv', kc)

    scale = 192.0 ** -0.5
    for h in range(NH):
        linear(k, w_uq, 768, [(h * 192, 128)], src_cq,
               lambda f0, fn, t0, tn, ps, pk: psnorm(k, ps, pk, 128, tn, cvs(k, f'mla_qn_n{ib}'), qnT[:, t0:t0 + tn], 'qnT', t0 > 0, rot), banks=rot)
        linear(k, w_ukv, 512, [(h * 256, 128)], src_ckv,
               lambda f0, fn, t0, tn, ps, pk: psnorm(k, ps, pk, 128, tn, cvs(k, f'mla_kn_n{ib}'), knT[:, t0:t0 + tn], 'knT', t0 > 0, rot), banks=rot)

        def cons_qpe(f0, fn, t0, tn, ps, pk):
            psnorm(k, ps, pk, 64, tn, cvs(k, f'mla_qn_r{ib}')[:64, :], qpn[:64, t0:t0 + tn], ('qpn', t0), False, rot)
            rope(k, qpn[:64, t0:t0 + tn], ('qpn', t0), t0, tn, qrT[:64, t0:t0 + tn], 'qrT', t0 > 0, rot)
        linear(k, w_uq, 768, [(h * 192 + 128, 64)], src_cq, cons_qpe, banks=rot)

        def cons_v(f0, fn, t0, tn, ps, pk):
            m = [t[0] for t in TK].index(t0)
            p.add('act', lambda e: e.activation(out=vaug[:tn, m, 0:128], in_=ps, func=ACT.Copy), reads=[pk], writes=['vaug'], acc=True)
        linear_tm(k, w_ukv, 512, [(h * 256 + 128, 128)], src_ckv, cons_v, banks=rot)

        for bnk in range(6):
            for q4 in range(4):
                p.add('pe', lambda e, bnk=bnk, q4=q4: e.matmul(k.ps[bnk][:, q4 * 128:(q4 + 1) * 128], lhsT=k.zb[:, :128], rhs=k.zb[:, :128], start=(q4 == 0), stop=True, skip_group_check=True),
                      reads=['zb'], writes=[('ps', bnk)], acc=(q4 > 0))
        for m, (k0, kn) in enumerate(TK):
            pT = pTs[m % 2]
            pkey = ('pT', m % 2)
            for q0 in range(k0, L, 512):
                qn_ = min(512, L - q0)
                ps, pk = k.ps_next([6, 7])
                p.add('pe', lambda e, ps=ps, k0=k0, kn=kn, q0=q0, qn_=qn_: e.matmul(ps[:kn, :qn_], lhsT=knT[:, k0:k0 + kn], rhs=qnT[:, q0:q0 + qn_], start=True, stop=False),
                      reads=['knT', 'qnT'], writes=[pk])
                p.add('pe', lambda e, ps=ps, k0=k0, kn=kn, q0=q0, qn_=qn_: e.matmul(ps[:kn, :qn_], lhsT=krT[:64, k0:k0 + kn], rhs=qrT[:64, q0:q0 + qn_], start=False, stop=True),
                      reads=['krT', 'qrT'], writes=[pk], acc=True)
                p.add('act', lambda e, ps=ps, kn=kn, q0=q0, qn_=qn_, pT=pT: e.activation(out=pT[:kn, q0:q0 + qn_], in_=ps[:kn, :qn_], func=ACT.Exp, scale=scale),
                      reads=[pk], writes=[pkey], acc=(q0 > k0))
            p.add('dve', lambda e, k0=k0, kn=kn, pT=pT: e.tensor_tensor(out=pT[:kn, k0:k0 + kn], in0=pT[:kn, k0:k0 + kn], in1=k.trib[:kn, :kn], op=ALU.mult),
                  reads=[pkey, 'trib'], writes=[pkey])
            for j in range(m, 17):
                t0, tq = TK[j]
                bnk = j // 3 if j < 15 else 5
                col = (j % 3) * 130 if j < 15 else (j - 15) * 130
                p.add('pe', lambda e, bnk=bnk, col=col, tq=tq, kn=kn, t0=t0, m=m, j=j, pT=pT: e.matmul(k.ps[bnk][:tq, col:col + 129], lhsT=pT[:kn, t0:t0 + tq], rhs=vaug[:kn, m, 0:129],
                                                                                                        start=False, stop=(m == j), skip_group_check=True),
                      reads=[pkey, 'vaug'], writes=[('ps', bnk)], acc=True)
        for j, (t0, tq) in enumerate(TK):
            bnk = j // 3 if j < 15 else 5
            col = (j % 3) * 130 if j < 15 else (j - 15) * 130
            i = j % 2
            acc_ap = k.ps[bnk]
            p.add('dve', lambda e, j=j, tq=tq, col=col, acc_ap=acc_ap: e.reciprocal(out=rcp[:tq, j:j + 1], in_=acc_ap[:tq, col + 128:col + 129]), reads=[('ps', bnk)], writes=[('rcp', j)])
            p.add('dve', lambda e, j=j, tq=tq, col=col, acc_ap=acc_ap, i=i: e.tensor_scalar(out=otm[i][:tq, :], in0=acc_ap[:tq, col:col + 128], scalar1=rcp[:tq, j:j + 1], scalar2=None, op0=ALU.mult),
                  reads=[('ps', bnk), ('rcp', j)], writes=[('otm', i)])
            ps, pk = k.ps_next([6, 7])
            psb = ps.bitcast(BF16)
            p.add('pe', lambda e, psb=psb, i=i, tq=tq: e.transpose(psb[:, :tq], otm[i][:tq, :], k.identb[:tq, :tq]), reads=[('otm', i), 'identb'], writes=[pk])
            p.add('act', lambda e, psb=psb, h=h, t0=t0, tq=tq: e.activation(out=oT[:, h, t0:t0 + tq], in_=psb[:, :tq], func=ACT.Copy), reads=[pk], writes=[('oT', h)], acc=(j > 0))
    p.barrier()
    proj_residual(k, w_out, 2048, lambda kc, t0, tn: (oT[:, kc, t0:t0 + tn], ('oT', kc)))


def conv_silu(k, raw, rkey, acc, akey, out_ap, okey, w4, bias):
    p = k.p
    if bias is not None:
        p.add('dve', lambda e: e.tensor_scalar(out=acc, in0=raw, scalar1=w4[3], scalar2=bias, op0=ALU.mult, op1=ALU.add), reads=[rkey, 'cv'], writes=[akey])
    else:
        p.add('dve', lambda e: e.tensor_scalar(out=acc, in0=raw, scalar1=w4[3], scalar2=None, op0=ALU.mult), reads=[rkey, 'cv'], writes=[akey])
    for sh in (1, 2, 3):
        p.add('dve', lambda e, sh=sh: e.scalar_tensor_tensor(out=acc[:, sh:], in0=raw[:, :L - sh], scalar=w4[3 - sh], in1=acc[:, sh:], op0=ALU.mult, op1=ALU.add),
              reads=[rkey, akey, 'cv'], writes=[akey])
    p.add('act', lambda e: e.activation(out=out_ap, in_=acc, func=ACT.Silu), reads=[akey], writes=[okey])


def linear_conv(k, w_ap, col, cidx, cwname, cbname, out_ap, okey):
    p = k.p
    raw, acc = k.hb[0], k.hb[1]

    def consume(f0, fn, t0, tn, ps, pk):
        p.add('act', lambda e: e.activation(out=raw[:, t0:t0 + tn], in_=ps, func=ACT.Copy), reads=[pk], writes=[('hb', 0)], acc=(t0 > 0))
    linear(k, w_ap, 2048, [(col, 128)], src_uT(k), consume)
    w4 = [cvs(k, f'{cwname}{j}', cidx, 1) for j in range(4)]
    bias = cvs(k, cbname, cidx, 1) if cbname else None
    conv_silu(k, raw, ('hb', 0), acc, ('hb', 1), out_ap, okey, w4, bias)


def out_proj_staged(k, w_out):
    p = k.p
    yTd = k.dr['yTd']
    p.barrier()
    yT = k.regA.bitcast(BF16).rearrange("p (c t) -> p c t", c=16)
    for half in range(2):
        for c in range(16):
            p.add('sp', lambda e, c=c, half=half: e.dma_start(out=yT[:, c, :], in_=yTd[half * 16 + c]), reads=[('yTd', half * 16 + c)], writes=[('yT', c)], dma=True)
        proj_residual(k, w_out[half * 2048:(half + 1) * 2048, :], 2048, lambda kc, t0, tn: (yT[:, kc, t0:t0 + tn], ('yT', kc)))


def ssm(k, ic, l):
    p, dr = k.p, k.dr
    rmsnorm_hT(k, f'norm_mix{l}')
    p.barrier()
    w_in, w_out = dr['ssm_w_in'][ic], dr['ssm_w_out'][ic]
    B = k.regB
    off = [0]

    def takef(n):
        ap = B[:, off[0]:off[0] + n]
        off[0] += n
        assert off[0] <= 16512, off[0]
        return ap

    def takeb(n):
        return takef((n + 1) // 2).bitcast(BF16)
    x_tm = takeb(17 * 512).rearrange("p (m f) -> p m f", m=17)
    z_tm = takeb(17 * 512).rearrange("p (m f) -> p m f", m=17)
    B_tm = takeb(17 * 128).rearrange("p (m f) -> p m f", m=17)
    BT = takeb(L)
    CT = takeb(L)
    xc = k.hb[0][:, :1032].bitcast(BF16)
    S = takef(512)
    S_bf = takeb(512)
    QKm = takef(128)
    dto = off[0]
    dtt = takef(17 * 8).rearrange("p (m h) -> p m h", m=17)
    aa = takef(17 * 8).rearrange("p (m h) -> p m h", m=17)
    G = takef(17 * 8).rearrange("p (m h) -> p m h", m=17)
    eG = takef(17 * 8).rearrange("p (m h) -> p m h", m=17)
    decl = takef(17 * 8).rearrange("p (m h) -> p m h", m=17)
    eGl = takef(17 * 8).rearrange("p (m h) -> p m h", m=17)
    Aneg = takef(64)
    onesf = takef(128)
    P4 = [takeb(512).rearrange("p (h i) -> p h i", h=4) for _ in range(2)]
    xdt = takeb(512)
    xdd = takeb(512)
    y = takef(512)
    yn = takeb(512)
    ystage = takeb(512).rearrange("p (c t) -> p c t", c=4)
    ss = takef(4)
    E = k.tmp[0].rearrange("p (h i) -> p h i", h=4)
    diag4 = k.tmp[1].rearrange("p (h i) -> p h i", h=4)
    yoff = k.tmp[2]
    trif = k.regC[:, 0:128]
    p.add('sp', lambda e: e.dma_start(out=trif, in_=dr['tri']), writes=['trif'], dma=True)
    yTd = dr['yTd']
    p.add('dve', lambda e: e.memset(onesf, 1.0), writes=['onesf'])
    p.add('act', lambda e: e.activation(out=Aneg, in_=cvs(k, f'ssm_alog_b{ic}'), func=ACT.Exp), reads=['cv'], writes=['Aneg'])
    p.add('dve', lambda e: e.tensor_scalar(out=Aneg, in0=Aneg, scalar1=-1.0, scalar2=None, op0=ALU.mult), reads=['Aneg'], writes=['Aneg'])
    dtb = cvs(k, f'ssm_dtb_b{ic}')
    Db = cvs(k, f'ssm_d_b{ic}')
    def group(g):
        hs = slice(g * 8, g * 8 + 8)

        def cons_dt(f0, fn, t0, tn, ps, pk):
            m = [t[0] for t in TK].index(t0)
            p.add('dve', lambda e: e.tensor_tensor(out=dtt[:tn, m, :], in0=ps, in1=dtb[:tn, hs], op=ALU.add), reads=[pk, 'cv'], writes=[('dt', m)])
            p.add('act', lambda e: e.activation(out=dtt[:tn, m, :], in_=dtt[:tn, m, :], func=ACT.Exp), reads=[('dt', m)], writes=[('dt', m)])
            p.add('dve', lambda e: e.tensor_scalar(out=dtt[:tn, m, :], in0=dtt[:tn, m, :], scalar1=1.0, scalar2=None, op0=ALU.add), reads=[('dt', m)], writes=[('dt', m)])
            p.add('act', lambda e: e.activation(out=dtt[:tn, m, :], in_=dtt[:tn, m, :], func=ACT.Ln), reads=[('dt', m)], writes=[('dt', m)])
            p.add('dve', lambda e: e.tensor_tensor(out=aa[:tn, m, :], in0=dtt[:tn, m, :], in1=Aneg[:tn, hs], op=ALU.mult), reads=[('dt', m), 'Aneg'], writes=[('aa', m)])
            ps2, pk2 = k.ps_next()
            p.add('pe', lambda e: e.matmul(ps2[:, 8:16], lhsT=onesf[:tn, :], rhs=aa[:tn, m, :], start=True, stop=True, skip_group_check=True), reads=[('aa', m), 'onesf'], writes=[pk2])
            p.add('pe', lambda e: e.matmul(ps2[:tn, 0:8], lhsT=trif[:tn, :tn], rhs=aa[:tn, m, :], start=False, stop=True, skip_group_check=True), reads=[('aa', m), 'trif'], writes=[pk2], acc=True)
            p.add('act', lambda e: e.activation(out=G[:tn, m, :], in_=ps2[:tn, 0:8], func=ACT.Copy), reads=[pk2], writes=[('G', m)])
            p.add('act', lambda e: e.activation(out=eG[:tn, m, :], in_=ps2[:tn, 0:8], func=ACT.Exp), reads=[pk2], writes=[('G', m)], acc=True)
            p.add('act', lambda e: e.activation(out=eGl[:, m, :], in_=ps2[:, 8:16], func=ACT.Exp), reads=[pk2], writes=[('G', m)], acc=True)
            p.add('dve', lambda e: e.tensor_tensor(out=decl[:tn, m, :], in0=ps2[:tn, 8:16], in1=G[:tn, m, :], op=ALU.subtract), reads=[pk2, ('G', m)], writes=[('decl', m)])
            p.add('act', lambda e: e.activation(out=decl[:tn, m, :], in_=decl[:tn, m, :], func=ACT.Exp), reads=[('decl', m)], writes=[('decl', m)])
        linear_tm(k, w_in, 2048, [(10240 + g * 8, 8)], src_uT(k), cons_dt, slot=(10240, 10304))

        linear_conv(k, w_in, 8192 + g * 128, 32 + g, f'ssm_cw{ic}_', f'ssm_cb{ic}', BT, 'BT')
        linear_conv(k, w_in, 9216 + g * 128, 40 + g, f'ssm_cw{ic}_', f'ssm_cb{ic}', CT, 'CT')

        def tr_to(srcT, skey, dst, dkey, c0):
            for m, (t0, tn) in enumerate(TK):
                ps, pk = k.ps_next()
                psb = ps.bitcast(BF16)
                p.add('pe', lambda e, psb=psb, t0=t0, tn=tn: e.transpose(psb[:tn, :128], srcT[:, t0:t0 + tn], k.identb), reads=[skey, 'identb'], writes=[pk])
                p.add('act', lambda e, psb=psb, m=m, tn=tn: e.activation(out=dst[:tn, m, c0:c0 + 128], in_=psb[:tn, :128], func=ACT.Copy), reads=[pk], writes=[(dkey, m)], acc=True)
        tr_to(BT, 'BT', B_tm, 'B_tm', 0)
        for cc in range(4):
            linear_conv(k, w_in, 4096 + g * 512 + cc * 128, g * 4 + cc, f'ssm_cw{ic}_', f'ssm_cb{ic}', xc, ('hb', 0))
            tr_to(xc, ('hb', 0), x_tm, 'x_tm', cc * 128)

        def cons_z(f0, fn, t0, tn, ps, pk):
            m = [t[0] for t in TK].index(t0)
            p.add('act', lambda e: e.activation(out=z_tm[:tn, m, :], in_=ps, func=ACT.Silu), reads=[pk], writes=[('z_tm', m)])
        linear_tm(k, w_in, 2048, [(g * 512, 512)], src_uT(k), cons_z)

        p.add('dve', lambda e: e.memset(S, 0.0), writes=['S'])
        p.add('dve', lambda e: e.memset(S_bf, 0.0), writes=['S_bf'])

        def chunk(m, t0, n):
            bc = lambda ap, d: ap.unsqueeze(2).to_broadcast([n, ap.shape[1], d])
            x3 = x_tm[:n, m, :].rearrange("p (h d) -> p h d", h=8)
            ps, pk = k.ps_next()
            p.add('pe', lambda e: e.matmul(ps[:n, :n], lhsT=BT[:, t0:t0 + n], rhs=CT[:, t0:t0 + n], start=True, stop=True), reads=['BT', 'CT'], writes=[pk])
            p.add('dve', lambda e: e.tensor_tensor(out=QKm[:n, :n], in0=ps[:n, :n], in1=trif[:n, :n], op=ALU.mult), reads=[pk, 'trif'], writes=['QKm'])
            p.add('pool', lambda e: e.tensor_tensor(out=xdt[:n, :].rearrange("p (h d) -> p h d", h=8), in0=x3, in1=bc(dtt[:n, m, :], 64), op=ALU.mult),
                  reads=[('x_tm', m), ('dt', m)], writes=['xdt'])
            p.add('pool', lambda e: e.tensor_tensor(out=xdd[:n, :].rearrange("p (h d) -> p h d", h=8), in0=xdt[:n, :].rearrange("p (h d) -> p h d", h=8), in1=bc(decl[:n, m, :], 64), op=ALU.mult),
                  reads=['xdt', ('decl', m)], writes=['xdd'])
            psA, pkA = k.ps_next()
            def quad(hb_):
                Pq = P4[hb_]
                h4 = slice(hb_ * 4, hb_ * 4 + 4)
                p.add('dve', lambda e: e.tensor_tensor(out=diag4[:n, :, :n], in0=k.identf[:n, :n].unsqueeze(1).to_broadcast([n, 4, n]), in1=bc(G[:n, m, h4], n), op=ALU.mult),
                      reads=['identf', ('G', m)], writes=['diag4'])
                psG, pkG = k.ps_next()
                for hh in range(4):
                    p.add('pe', lambda e, hh=hh: e.matmul(psG[:n, hh * 128:hh * 128 + n], lhsT=onesf[:n, :n], rhs=diag4[:n, hh, :n], start=(hh == 0), stop=True, skip_group_check=True),
                          reads=['onesf', 'diag4'], writes=[pkG], acc=(hh > 0))
                p.add('dve', lambda e: e.tensor_tensor(out=E[:n, :, :n], in0=psG.rearrange("p (h i) -> p h i", h=4)[:n, :, :n], in1=bc(G[:n, m, h4], n), op=ALU.subtract),
                      reads=[pkG, ('G', m)], writes=['E'])
                p.add('dve', lambda e: e.tensor_scalar(out=E[:n, :, :n], in0=E[:n, :, :n], scalar1=0.0, scalar2=None, op0=ALU.min), reads=['E'], writes=['E'])
                p.add('act', lambda e: e.activation(out=E[:n, :, :n], in_=E[:n, :, :n], func=ACT.Exp), reads=['E'], writes=['E'])
                p.add('dve', lambda e, Pq=Pq: e.scalar_tensor_tensor(out=Pq[:n, :, :n], in0=E[:n, :, :n], scalar=1.0, in1=QKm[:n, :n].unsqueeze(1).to_broadcast([n, 4, n]),
                                                                    op0=ALU.min, op1=ALU.mult), reads=['E', 'QKm'], writes=[('P4', hb_)])
                for hh in range(4):
                    h = hb_ * 4 + hh
                    p.add('pe', lambda e, hh=hh, h=h, Pq=Pq: e.matmul(psA[:n, h * 64:(h + 1) * 64], lhsT=Pq[:n, hh, :n], rhs=xdt[:n, h * 64:(h + 1) * 64], start=(h == 0), stop=True, skip_group_check=True),
                          reads=[('P4', hb_), 'xdt'], writes=[pkA], acc=(h > 0))
            quad(0)
            quad(1)
            psB, pkB = k.ps_next()
            p.add('pe', lambda e: e.matmul(psB[:n, :], lhsT=CT[:, t0:t0 + n], rhs=S_bf, start=True, stop=True), reads=['CT', 'S_bf'], writes=[pkB])
            p.add('dve', lambda e: e.tensor_tensor(out=yoff[:n, :].rearrange("p (h d) -> p h d", h=8), in0=psB[:n, :].rearrange("p (h d) -> p h d", h=8), in1=bc(eG[:n, m, :], 64), op=ALU.mult),
                  reads=[pkB, ('G', m)], writes=['yoff'])
            p.add('dve', lambda e: e.tensor_tensor(out=y[:n, :], in0=psA[:n, :], in1=yoff[:n, :], op=ALU.add), reads=[pkA, 'yoff'], writes=['y'])
            p.add('pool', lambda e: e.tensor_tensor(out=yoff[:n, :].rearrange("p (h d) -> p h d", h=8), in0=x3, in1=bc(Db[:n, hs], 64), op=ALU.mult),
                  reads=[('x_tm', m), 'cv', 'y'], writes=['yoff'])
            p.add('dve', lambda e: e.tensor_tensor(out=y[:n, :], in0=y[:n, :], in1=yoff[:n, :], op=ALU.add), reads=['y', 'yoff'], writes=['y'])
            psS, pkS = k.ps_next()
            p.add('pe', lambda e: e.matmul(psS[:, :], lhsT=B_tm[:n, m, :], rhs=xdd[:n, :], start=True, stop=True), reads=[('B_tm', m), 'xdd'], writes=[pkS])
            p.add('dve', lambda e: e.tensor_tensor(out=S.rearrange("p (h d) -> p h d", h=8), in0=S.rearrange("p (h d) -> p h d", h=8),
                                                   in1=eGl[:, m, :].unsqueeze(2).to_broadcast([128, 8, 64]), op=ALU.mult), reads=['S', ('G', m)], writes=['S'])
            p.add('dve', lambda e: e.tensor_tensor(out=S, in0=S, in1=psS[:, :], op=ALU.add), reads=['S', pkS], writes=['S'])
            p.add('act', lambda e: e.activation(out=S_bf, in_=S, func=ACT.Copy), reads=['S'], writes=['S_bf'])
            p.add('dve', lambda e: e.tensor_tensor(out=y[:n, :], in0=y[:n, :], in1=z_tm[:n, m, :], op=ALU.mult), reads=['y', ('z_tm', m)], writes=['y'])
            p.add('dve', lambda e: e.memset(ss[:n, 0:1], 0.0), writes=['ss'])
            p.add('act', lambda e: e.activation(out=yoff[:n, :], in_=y[:n, :], func=ACT.Square, accum_out=ss[:n, 0:1]), reads=['y', 'yoff', 'ss'], writes=['yoff', 'ss'])
            p.add('dve', lambda e: e.tensor_scalar(out=ss[:n, 1:2], in0=ss[:n, 0:1], scalar1=1.0 / 512, scalar2=EPS, op0=ALU.mult, op1=ALU.add), reads=['ss'], writes=['ss'])
            p.add('act', lambda e: e.activation(out=ss[:n, 1:2], in_=ss[:n, 1:2], func=ACT.Sqrt), reads=['ss'], writes=['ss'])
            p.add('dve', lambda e: e.reciprocal(out=ss[:n, 2:3], in_=ss[:n, 1:2]), reads=['ss'], writes=['ss'])
            p.add('dve', lambda e: e.tensor_scalar(out=yn[:n, :], in0=y[:n, :], scalar1=ss[:n, 2:3], scalar2=None, op0=ALU.mult), reads=['y', 'ss'], writes=['yn'])
            for cc in range(4):
                pst, pkt = k.ps_next()
                pstb = pst.bitcast(BF16)
                p.add('pe', lambda e, cc=cc, pstb=pstb: e.transpose(pstb[:, :n], yn[:n, cc * 128:(cc + 1) * 128], k.identb[:n, :n]), reads=['yn', 'identb'], writes=[pkt])
                p.add('act', lambda e, cc=cc, pstb=pstb: e.activation(out=ystage[:, cc, :n], in_=pstb[:, :n], func=ACT.Copy, scale=cvs(k, f'ssm_norm{ic}', g * 4 + cc, 1)),
                      reads=[pkt, 'cv'], writes=[('ystage', cc)])
                p.add('sp', lambda e, cc=cc: e.dma_start(out=yTd[g * 4 + cc][:, t0:t0 + n], in_=ystage[:, cc, :n]), reads=[('ystage', cc)], writes=[('yTd', g * 4 + cc)], acc=True, dma=True)
        for m, (t0, n) in enumerate(TK):
            chunk(m, t0, n)
            if g == 0 and m == 1:
                dump(k, 'QKm', QKm, 'QKm')
                dump(k, 'E', k.tmp[0], 'E')
                dump(k, 'y', y, 'y')
                dump(k, 'yn', yn, 'yn', BF16)
                dump(k, 'S', S, 'S')
                dump(k, 'xdt', xdt, 'xdt', BF16)
                dump(k, 'ss', ss, 'ss')
        if g == 0:
            dump(k, 'dtt', B[:, dto:dto + 136], ('dt', 16))
            dump(k, 'G', B[:, dto + 272:dto + 408], ('G', 16))
            dump(k, 'eGl', B[:, dto + 680:dto + 816], ('G', 16))
            dump(k, 'decl', B[:, dto + 544:dto + 680], ('decl', 16))
            dump(k, 'xtm', x_tm, ('x_tm', 16), BF16)
            dump(k, 'ztm', z_tm, ('z_tm', 16), BF16)
            dump(k, 'BT', BT, 'BT', BF16)
            dump(k, 'CT', CT, 'CT', BF16)
            dump(k, 'Btm', B_tm, ('B_tm', 16), BF16)
    for g_ in range(8):
        group(g_)
    out_proj_staged(k, w_out)


CH = [(0, 16)] + [(16 + 64 * i, 64) for i in range(32)]


def gdn(k, ia, l):
    p, dr = k.p, k.dr
    rmsnorm_hT(k, f'norm_mix{l}')
    p.barrier()
    w_in, w_out = dr['gdn_w_in'][ia], dr['gdn_w_out'][ia]
    B = k.regB
    off = [0]

    def takef(n):
        ap = B[:, off[0]:off[0] + n]
        off[0] += n
        assert off[0] <= 16512, off[0]
        return ap

    def takeb(n):
        return takef((n + 1) // 2).bitcast(BF16)
    NC = 33
    k_tm = takeb(NC * 128).rearrange("p (m f) -> p m f", m=NC)
    v_tm = takeb(NC * 256).rearrange("p (m f) -> p m f", m=NC)
    z_tm = takeb(NC * 256).rearrange("p (m f) -> p m f", m=NC)
    qT = takeb(L)
    kT = takeb(L)
    S = takef(256)
    S_bf = takeb(256)
    bet = takef(NC * 2).rearrange("p (m h) -> p m h", m=NC)
    gg = takef(NC * 2).rearrange("p (m h) -> p m h", m=NC)
    G = takef(NC * 2).rearrange("p (m h) -> p m h", m=NC)
    eG = takef(NC * 2).rearrange("p (m h) -> p m h", m=NC)
    decl = takef(NC * 2).rearrange("p (m h) -> p m h", m=NC)
    eGl = takef(NC * 2).rearrange("p (m h) -> p m h", m=NC)
    Aneg = takef(32)
    onesf = takef(128)
    mask2 = takef(128).rearrange("p (a i) -> p a i", a=2)
    KQm = takef(128).rearrange("p (a i) -> p a i", a=2)
    E = takef(128).rearrange("p (h i) -> p h i", h=2)
    diag2 = takef(128).rearrange("p (h i) -> p h i", h=2)
    Pqs = [takeb(128).rearrange("p (h i) -> p h i", h=2) for _ in range(2)]
    Pqs.append(k.regC[:, 1152:1216].bitcast(BF16).rearrange("p (h i) -> p h i", h=2))
    Pm = [takef(128).rearrange("p (h i) -> p h i", h=2) for _ in range(2)]
    PTm = [takef(128).rearrange("p (h i) -> p h i", h=2) for _ in range(2)]
    Z = takef(128).rearrange("p (h i) -> p h i", h=2)
    Zbs = [takeb(128).rearrange("p (h i) -> p h i", h=2) for _ in range(2)]
    Zbs.append(k.regC[:, 1216:1280].bitcast(BF16).rearrange("p (h i) -> p h i", h=2))
    def rc(i):
        return k.regC[:, 128 + 128 * i:256 + 128 * i].rearrange("p (h i) -> p h i", h=2)
    KQms = [KQm, rc(0)]
    Es = [E, rc(1)]
    diag2s = [diag2, rc(2)]
    Pms = [Pm, [rc(3), rc(4)]]
    PTms = [PTm, [rc(5), rc(6)]]
    Zs = [Z, rc(7)]
    Rr = takeb(256)
    vnew = takeb(256)
    vnd = takeb(256)
    on = takeb(256)
    ostage = takeb(128).rearrange("p (h t) -> p h t", h=2)
    ssq = takef(8)
    xc = k.hb[0][:, :1032].bitcast(BF16)
    t256 = k.tmp[0][:, :256]
    o_f = k.tmp[1][:, :256]
    sqj = k.tmp[2][:, :256]
    yTd = dr['yTd']
    trif = k.regC[:, 0:128]
    p.add('sp', lambda e: e.dma_start(out=trif, in_=dr['tri']), writes=['trif'], dma=True)
    p.add('dve', lambda e: e.memset(onesf, 1.0), writes=['onesf'])
    p.add('dve', lambda e: e.tensor_tensor(out=mask2[:64, 0, :], in0=trif[:64, :64], in1=k.identf[:64, :64], op=ALU.subtract), reads=['trif', 'identf'], writes=['mask2'])
    p.add('dve', lambda e: e.tensor_copy(out=mask2[:64, 1, :], in_=trif[:64, :64]), reads=['trif'], writes=['mask2'], acc=True)
    p.add('act', lambda e: e.activation(out=Aneg, in_=cvs(k, f'gdn_alog_b{ia}'), func=ACT.Exp), reads=['cv'], writes=['Aneg'])
    p.add('dve', lambda e: e.tensor_scalar(out=Aneg, in0=Aneg, scalar1=-1.0, scalar2=None, op0=ALU.mult), reads=['Aneg'], writes=['Aneg'])
    dtb = cvs(k, f'gdn_dtb_b{ia}')
    gnorm = cvs(k, f'gdn_norm{ia}')
    cwn = f'gdn_cw{ia}_'
    cidx = {t0: m for m, (t0, n) in enumerate(CH)}

    def l2norm_fm(xT, key, scale):
        def tile(t0, tn):
            i = k.tmpn % 3
            k.tmpn += 1
            sqt = k.tmp[i].bitcast(BF16)[:, :tn]
            p.add('act', lambda e: e.activation(out=sqt, in_=xT[:, t0:t0 + tn], func=ACT.Square), reads=[key], writes=[('tmp', i)])
            ps2, pk2 = k.ps_next()
            p.add('pe', lambda e: e.matmul(ps2[:, :tn], lhsT=k.onesb, rhs=sqt, start=True, stop=True), reads=[('tmp', i), 'onesb'], writes=[pk2])
            rs = k.tmp[i][:, :tn]
            p.add('dve', lambda e: e.tensor_scalar(out=rs, in0=ps2[:, :tn], scalar1=1.0 / (scale * scale), scalar2=EPS / (scale * scale), op0=ALU.mult, op1=ALU.add),
                  reads=[pk2], writes=[('tmp', i)])
            p.add('act', lambda e: e.activation(out=rs, in_=rs, func=ACT.Sqrt), reads=[('tmp', i)], writes=[('tmp', i)])
            p.add('dve', lambda e: e.reciprocal(out=rs, in_=rs), reads=[('tmp', i)], writes=[('tmp', i)])
            p.add('dve', lambda e: e.tensor_tensor(out=xT[:, t0:t0 + tn], in0=xT[:, t0:t0 + tn], in1=rs, op=ALU.mult), reads=[key, ('tmp', i)], writes=[key])
        for (t0, tn) in TT:
            tile(t0, tn)

    def tr_to(srcT, skey, dst, dkey, c0):
        def one(m, t0, tn):
            ps, pk = k.ps_next()
            psb = ps.bitcast(BF16)
            p.add('pe', lambda e: e.transpose(psb[:tn, :128], srcT[:, t0:t0 + tn], k.identb), reads=[skey, 'identb'], writes=[pk])
            p.add('act', lambda e: e.activation(out=dst[:tn, m, c0:c0 + 128], in_=psb[:tn, :128], func=ACT.Copy), reads=[pk], writes=[(dkey, m)], acc=True)
        for m, (t0, tn) in enumerate(CH):
            one(m, t0, tn)

    def group(hq):
        hs = slice(2 * hq, 2 * hq + 2)

        def cons_b(f0, fn, t0, tn, ps, pk):
            m = cidx[t0]
            p.add('act', lambda e: e.activation(out=bet[:tn, m, :], in_=ps, func=ACT.Sigmoid), reads=[pk], writes=[('bet', m)])
        linear_tm(k, w_in, 2048, [(12288 + 2 * hq, 2)], src_uT(k), cons_b, tiles=CH, slot=(12288, 12352))

        def cons_a(f0, fn, t0, tn, ps, pk):
            m = cidx[t0]
            p.add('dve', lambda e: e.tensor_tensor(out=gg[:tn, m, :], in0=ps, in1=dtb[:tn, hs], op=ALU.add), reads=[pk, 'cv'], writes=[('gg', m)])
            p.add('act', lambda e: e.activation(out=gg[:tn, m, :], in_=gg[:tn, m, :], func=ACT.Exp), reads=[('gg', m)], writes=[('gg', m)])
            p.add('dve', lambda e: e.tensor_scalar(out=gg[:tn, m, :], in0=gg[:tn, m, :], scalar1=1.0, scalar2=None, op0=ALU.add), reads=[('gg', m)], writes=[('gg', m)])
            p.add('act', lambda e: e.activation(out=gg[:tn, m, :], in_=gg[:tn, m, :], func=ACT.Ln), reads=[('gg', m)], writes=[('gg', m)])
            p.add('dve', lambda e: e.tensor_tensor(out=gg[:tn, m, :], in0=gg[:tn, m, :], in1=Aneg[:tn, hs], op=ALU.mult), reads=[('gg', m), 'Aneg'], writes=[('gg', m)])
            ps2, pk2 = k.ps_next()
            p.add('pe', lambda e: e.matmul(ps2[:, 8:10], lhsT=onesf[:tn, :], rhs=gg[:tn, m, :], start=True, stop=True, skip_group_check=True), reads=[('gg', m), 'onesf'], writes=[pk2])
            p.add('pe', lambda e: e.matmul(ps2[:tn, 0:2], lhsT=trif[:tn, :tn], rhs=gg[:tn, m, :], start=False, stop=True, skip_group_check=True), reads=[('gg', m), 'trif'], writes=[pk2], acc=True)
            p.add('act', lambda e: e.activation(out=G[:tn, m, :], in_=ps2[:tn, 0:2], func=ACT.Copy), reads=[pk2], writes=[('G', m)])
            p.add('act', lambda e: e.activation(out=eG[:tn, m, :], in_=ps2[:tn, 0:2], func=ACT.Exp), reads=[pk2], writes=[('G', m)], acc=True)
            p.add('act', lambda e: e.activation(out=eGl[:, m, :], in_=ps2[:, 8:10], func=ACT.Exp), reads=[pk2], writes=[('G', m)], acc=True)
            p.add('dve', lambda e: e.tensor_tensor(out=decl[:tn, m, :], in0=ps2[:tn, 8:10], in1=G[:tn, m, :], op=ALU.subtract), reads=[pk2, ('G', m)], writes=[('decl', m)])
            p.add('act', lambda e: e.activation(out=decl[:tn, m, :], in_=decl[:tn, m, :], func=ACT.Exp), reads=[('decl', m)], writes=[('decl', m)])
        linear_tm(k, w_in, 2048, [(12320 + 2 * hq, 2)], src_uT(k), cons_a, tiles=CH, slot=(12288, 12352))

        linear_conv(k, w_in, hq * 128, hq, cwn, None, qT, 'qT')
        l2norm_fm(qT, 'qT', 128.0 ** -0.5)
        linear_conv(k, w_in, 2048 + hq * 128, 16 + hq, cwn, None, kT, 'kT')
        l2norm_fm(kT, 'kT', 1.0)
        tr_to(kT, 'kT', k_tm, 'k_tm', 0)
        for r in range(2):
            linear_conv(k, w_in, 4096 + (2 * hq + r) * 128, 32 + 2 * hq + r, cwn, None, xc, ('hb', 0))
            tr_to(xc, ('hb', 0), v_tm, 'v_tm', r * 128)

        def cons_z(f0, fn, t0, tn, ps, pk):
            m = cidx[t0]
            p.add('act', lambda e: e.activation(out=z_tm[:tn, m, :], in_=ps, func=ACT.Silu), reads=[pk], writes=[('z_tm', m)])
        linear_tm(k, w_in, 2048, [(8192 + 2 * hq * 128, 256)], src_uT(k), cons_z, tiles=CH)

        p.add('dve', lambda e: e.memset(S, 0.0), writes=['S'])
        p.add('dve', lambda e: e.memset(S_bf, 0.0), writes=['S_bf'])

        def inv(m, t0, n):
            Pq = Pqs[m % 3]
            Zb = Zbs[m % 3]
            PqK = ('Pq', m % 3)
            ZbK = ('Zb', m % 3)
            pp = m % 2
            ibanks = [0, 1, 2] if pp == 0 else [3, 4, 5]
            bcnt = [0]

            def nb():
                b_ = ibanks[bcnt[0] % 3]
                bcnt[0] += 1
                return k.ps[b_], ('ps', b_)
            KQm, E, diag2, Pm, PTm, Z = KQms[pp], Es[pp], diag2s[pp], Pms[pp], PTms[pp], Zs[pp]
            bc = lambda ap, d: ap.unsqueeze(2).to_broadcast([n, ap.shape[1], d])
            ps, pk = nb()
            p.add('pe', lambda e: e.matmul(ps[:n, 0:n], lhsT=kT[:, t0:t0 + n], rhs=kT[:, t0:t0 + n], start=True, stop=True, skip_group_check=True), reads=['kT'], writes=[pk])
            yield
            p.add('pe', lambda e: e.matmul(ps[:n, 64:64 + n], lhsT=kT[:, t0:t0 + n], rhs=qT[:, t0:t0 + n], start=False, stop=True, skip_group_check=True), reads=['kT', 'qT'], writes=[pk], acc=True)
            yield
            p.add('dve', lambda e: e.tensor_tensor(out=KQm[:n, :, :n], in0=ps[:n, 0:128].rearrange("p (a i) -> p a i", a=2)[:, :, :n], in1=mask2[:n, :, :n], op=ALU.mult),
                  reads=[pk, 'mask2'], writes=[('KQm', pp)])
            yield
            p.add('dve', lambda e: e.tensor_tensor(out=diag2[:n, :, :n], in0=k.identf[:n, :n].unsqueeze(1).to_broadcast([n, 2, n]), in1=bc(G[:n, m, :], n), op=ALU.mult),
                  reads=['identf', ('G', m)], writes=[('diag2', pp)])
            yield
            psG, pkG = nb()
            for hh in range(2):
                p.add('pe', lambda e, hh=hh: e.matmul(psG[:n, hh * 64:hh * 64 + n], lhsT=onesf[:n, :n], rhs=diag2[:n, hh, :n], start=(hh == 0), stop=True, skip_group_check=True),
                      reads=['onesf', ('diag2', pp)], writes=[pkG], acc=(hh > 0))
            yield
            p.add('dve', lambda e: e.tensor_tensor(out=E[:n, :, :n], in0=psG[:n, 0:128].rearrange("p (h i) -> p h i", h=2)[:, :, :n], in1=bc(G[:n, m, :], n), op=ALU.subtract),
                  reads=[pkG, ('G', m)], writes=[('E', pp)])
            yield
            p.add('dve', lambda e: e.tensor_scalar(out=E[:n, :, :n], in0=E[:n, :, :n], scalar1=0.0, scalar2=None, op0=ALU.min), reads=[('E', pp)], writes=[('E', pp)])
            yield
            p.add('act', lambda e: e.activation(out=E[:n, :, :n], in_=E[:n, :, :n], func=ACT.Exp), reads=[('E', pp)], writes=[('E', pp)])
            yield
            p.add('pool', lambda e: e.tensor_tensor(out=Pq[:n, :, :n], in0=E[:n, :, :n], in1=KQm[:n, 1, :n].unsqueeze(1).to_broadcast([n, 2, n]), op=ALU.mult),
                  reads=[('E', pp), ('KQm', pp)], writes=[PqK])
            yield
            M0 = Pm[0]
            p.add('dve', lambda e: e.tensor_tensor(out=M0[:n, :, :n], in0=E[:n, :, :n], in1=KQm[:n, 0, :n].unsqueeze(1).to_broadcast([n, 2, n]), op=ALU.mult),
                  reads=[('E', pp), ('KQm', pp)], writes=[('Pm', pp, 0)])
            yield
            p.add('dve', lambda e: e.tensor_tensor(out=M0[:n, :, :n], in0=M0[:n, :, :n], in1=bc(bet[:n, m, :], n), op=ALU.mult), reads=[('Pm', pp, 0), ('bet', m)], writes=[('Pm', pp, 0)])
            yield
            p.add('dve', lambda e: e.tensor_tensor(out=Z[:n, :, :n], in0=k.identf[:n, :n].unsqueeze(1).to_broadcast([n, 2, n]), in1=M0[:n, :, :n], op=ALU.subtract),
                  reads=['identf', ('Pm', pp, 0)], writes=[('Z', pp)])
            yield
            pst, pkt = nb()
            for hh in range(2):
                p.add('pe', lambda e, hh=hh: e.transpose(pst[:n, hh * 64:hh * 64 + n], M0[:n, hh, :n], k.identf[:n, :n]), reads=[('Pm', pp, 0), 'identf'], writes=[pkt], acc=(hh > 0))
            yield
            p.add('act', lambda e: e.activation(out=PTm[0][:n, :, :n], in_=pst[:n, 0:128].rearrange("p (h i) -> p h i", h=2)[:, :, :n], func=ACT.Copy), reads=[pkt], writes=[('PTm', pp, 0)])

            def dbl(it):
                a, b = (it - 1) % 2, it % 2
                Pa, PTa, Pb, PTb = Pm[a], PTm[a], Pm[b], PTm[b]
                last = (it == 5)
                psT, pkT = nb()
                for hh in range(2):
                    p.add('pe', lambda e, hh=hh: e.matmul(psT[:n, hh * 64:hh * 64 + n], lhsT=Pa[:n, hh, :n], rhs=PTa[:n, hh, :n], start=(hh == 0), stop=True, skip_group_check=True),
                          reads=[('Pm', pp, a), ('PTm', pp, a)], writes=[pkT], acc=(hh > 0))
                yield
                p.add('act', lambda e: e.activation(out=PTb[:n, :, :n], in_=psT[:n, 0:128].rearrange("p (h i) -> p h i", h=2)[:, :, :n], func=ACT.Copy), reads=[pkT], writes=[('PTm', pp, b)])
                yield
                if not last:
                    psP, pkP = nb()
                    for hh in range(2):
                        p.add('pe', lambda e, hh=hh: e.matmul(psP[:n, hh * 64:hh * 64 + n], lhsT=PTa[:n, hh, :n], rhs=Pa[:n, hh, :n], start=(hh == 0), stop=True, skip_group_check=True),
                              reads=[('Pm', pp, a), ('PTm', pp, a)], writes=[pkP], acc=(hh > 0))
                    yield
                    p.add('dve', lambda e: e.tensor_copy(out=Pb[:n, :, :n], in_=psP[:n, 0:128].rearrange("p (h i) -> p h i", h=2)[:, :, :n]), reads=[pkP], writes=[('Pm', pp, b)])
                psZ, pkZ = nb()
                for hh in range(2):
                    p.add('pe', lambda e, hh=hh: e.matmul(psZ[:n, hh * 64:hh * 64 + n], lhsT=PTb[:n, hh, :n], rhs=Z[:n, hh, :n], start=(hh == 0), stop=True, skip_group_check=True),
                          reads=[('PTm', pp, b), ('Z', pp)], writes=[pkZ], acc=(hh > 0))
                yield
                p.add('dve', lambda e: e.tensor_tensor(out=Z[:n, :, :n], in0=Z[:n, :, :n], in1=psZ[:n, 0:128].rearrange("p (h i) -> p h i", h=2)[:, :, :n], op=ALU.add), reads=[('Z', pp), pkZ], writes=[('Z', pp)])
            for it in range(1, 6):
                yield from dbl(it)
                yield
            yield
            p.add('act', lambda e: e.activation(out=Zb[:n, :, :n], in_=Z[:n, :, :n], func=ACT.Copy), reads=[('Z', pp)], writes=[ZbK])
            yield
        def st(m, t0, n):
            Pq = Pqs[m % 3]
            Zb = Zbs[m % 3]
            PqK = ('Pq', m % 3)
            ZbK = ('Zb', m % 3)
            bcnt = [0]

            def nb():
                b_ = 6 + (bcnt[0] % 2)
                bcnt[0] += 1
                return k.ps[b_], ('ps', b_)
            bc = lambda ap, d: ap.unsqueeze(2).to_broadcast([n, ap.shape[1], d])
            psK, pkK = nb()
            p.add('pe', lambda e: e.matmul(psK[:n, :256], lhsT=kT[:, t0:t0 + n], rhs=S_bf, start=True, stop=True), reads=['kT', 'S_bf'], writes=[pkK])
            yield
            p.add('dve', lambda e: e.tensor_tensor(out=t256[:n, :].rearrange("p (h d) -> p h d", h=2), in0=psK[:n, :256].rearrange("p (h d) -> p h d", h=2), in1=bc(eG[:n, m, :], 128), op=ALU.mult),
                  reads=[pkK, ('G', m)], writes=['t256'])
            yield
            p.add('dve', lambda e: e.tensor_tensor(out=Rr[:n, :], in0=v_tm[:n, m, :], in1=t256[:n, :], op=ALU.subtract), reads=[('v_tm', m), 't256'], writes=['Rr'])
            yield
            psV, pkV = nb()
            for hh in range(2):
                p.add('pe', lambda e, hh=hh: e.matmul(psV[:n, hh * 128:(hh + 1) * 128], lhsT=Zb[:n, hh, :n], rhs=Rr[:n, hh * 128:(hh + 1) * 128], start=(hh == 0), stop=True, skip_group_check=True),
                      reads=[ZbK, 'Rr'], writes=[pkV], acc=(hh > 0))
            yield
            p.add('dve', lambda e: e.tensor_tensor(out=vnew[:n, :].rearrange("p (h d) -> p h d", h=2), in0=psV[:n, :256].rearrange("p (h d) -> p h d", h=2), in1=bc(bet[:n, m, :], 128), op=ALU.mult),
                  reads=[pkV, ('bet', m)], writes=['vnew'])
            yield
            p.add('pool', lambda e: e.tensor_tensor(out=vnd[:n, :].rearrange("p (h d) -> p h d", h=2), in0=vnew[:n, :].rearrange("p (h d) -> p h d", h=2), in1=bc(decl[:n, m, :], 128), op=ALU.mult),
                  reads=['vnew', ('decl', m)], writes=['vnd'])
            yield
            psA, pkA = nb()
            for hh in range(2):
                p.add('pe', lambda e, hh=hh: e.matmul(psA[:n, hh * 128:(hh + 1) * 128], lhsT=Pq[:n, hh, :n], rhs=vnew[:n, hh * 128:(hh + 1) * 128], start=(hh == 0), stop=True, skip_group_check=True),
                      reads=[PqK, 'vnew'], writes=[pkA], acc=(hh > 0))
            yield
            psB, pkB = nb()
            p.add('pe', lambda e: e.matmul(psB[:n, :256], lhsT=qT[:, t0:t0 + n], rhs=S_bf, start=True, stop=True), reads=['qT', 'S_bf'], writes=[pkB])
            yield
            p.add('dve', lambda e: e.tensor_tensor(out=t256[:n, :].rearrange("p (h d) -> p h d", h=2), in0=psB[:n, :256].rearrange("p (h d) -> p h d", h=2), in1=bc(eG[:n, m, :], 128), op=ALU.mult),
                  reads=[pkB, ('G', m)], writes=['t256'])
            yield
            p.add('dve', lambda e: e.tensor_tensor(out=o_f[:n, :], in0=psA[:n, :256], in1=t256[:n, :], op=ALU.add), reads=[pkA, 't256'], writes=['o_f'])
            yield
            psS, pkS = nb()
            p.add('pe', lambda e: e.matmul(psS[:, :256], lhsT=k_tm[:n, m, :], rhs=vnd[:n, :], start=True, stop=True), reads=[('k_tm', m), 'vnd'], writes=[pkS])
            yield
            p.add('dve', lambda e: e.tensor_tensor(out=S.rearrange("p (h d) -> p h d", h=2), in0=S.rearrange("p (h d) -> p h d", h=2),
                                                   in1=eGl[:, m, :].unsqueeze(2).to_broadcast([128, 2, 128]), op=ALU.mult), reads=['S', ('G', m)], writes=['S'])
            yield
            p.add('dve', lambda e: e.tensor_tensor(out=S, in0=S, in1=psS[:, :256], op=ALU.add), reads=['S', pkS], writes=['S'])
            yield
            p.add('act', lambda e: e.activation(out=S_bf, in_=S, func=ACT.Copy), reads=['S'], writes=['S_bf'])
            yield
            p.add('act', lambda e: e.activation(out=sqj[:n, :], in_=o_f[:n, :], func=ACT.Square), reads=['o_f'], writes=['sqj'])
            yield
            p.add('dve', lambda e: e.tensor_reduce(out=ssq[:n, 0:2], in_=sqj[:n, :].rearrange("p (h d) -> p h d", h=2), axis=AX.X, op=ALU.add), reads=['sqj'], writes=['ssq'])
            yield
            p.add('dve', lambda e: e.tensor_scalar(out=ssq[:n, 2:4], in0=ssq[:n, 0:2], scalar1=1.0 / 128, scalar2=EPS, op0=ALU.mult, op1=ALU.add), reads=['ssq'], writes=['ssq'])
            yield
            p.add('act', lambda e: e.activation(out=ssq[:n, 2:4], in_=ssq[:n, 2:4], func=ACT.Sqrt), reads=['ssq'], writes=['ssq'])
            yield
            p.add('dve', lambda e: e.reciprocal(out=ssq[:n, 4:6], in_=ssq[:n, 2:4]), reads=['ssq'], writes=['ssq'])
            yield
            p.add('dve', lambda e: e.tensor_tensor(out=o_f[:n, :].rearrange("p (h d) -> p h d", h=2), in0=o_f[:n, :].rearrange("p (h d) -> p h d", h=2), in1=bc(ssq[:n, 4:6], 128), op=ALU.mult),
                  reads=['o_f', 'ssq'], writes=['o_f'])
            yield
            p.add('dve', lambda e: e.tensor_tensor(out=on[:n, :], in0=o_f[:n, :], in1=z_tm[:n, m, :], op=ALU.mult), reads=['o_f', ('z_tm', m)], writes=['on'])
            yield
            for hh in range(2):
                pso, pko = nb()
                psob = pso.bitcast(BF16)
                p.add('pe', lambda e, hh=hh, psob=psob: e.transpose(psob[:, :n], on[:n, hh * 128:(hh + 1) * 128], k.identb[:n, :n]), reads=['on', 'identb'], writes=[pko])
                p.add('act', lambda e, hh=hh, psob=psob: e.activation(out=ostage[:, hh, :n], in_=psob[:, :n], func=ACT.Copy, scale=gnorm), reads=[pko, 'cv'], writes=[('ostage', hh)])
                p.add('sp', lambda e, hh=hh: e.dma_start(out=yTd[2 * hq + hh][:, t0:t0 + n], in_=ostage[:, hh, :n]), reads=[('ostage', hh)], writes=[('yTd', 2 * hq + hh)], acc=True, dma=True)
            yield
        NCH = len(CH)
        inflight = []
        done_inv = set()
        nxt = [0]
        cur = None
        cur_m = 0

        def start_invs():
            while len(inflight) < 2 and nxt[0] < NCH and nxt[0] <= cur_m + 2:
                mm = nxt[0]
                inflight.append([mm, inv(mm, CH[mm][0], CH[mm][1])])
                nxt[0] += 1
        start_invs()
        while cur_m < NCH:
            if cur is None and cur_m in done_inv:
                cur = st(cur_m, CH[cur_m][0], CH[cur_m][1])
            if cur is not None:
                try:
                    next(cur)
                except StopIteration:
                    cur = None
                    cur_m += 1
                    start_invs()
                    continue
            for item in list(inflight):
                try:
                    next(item[1])
                except StopIteration:
                    inflight.remove(item)
                    done_inv.add(item[0])
                    start_invs()
        assert not inflight and nxt[0] == NCH
    for hq_ in range(16):
        group(hq_)
    out_proj_staged(k, w_out)

def fm(v):
    v = np.asarray(v, np.float32)
    return np.ascontiguousarray(v.reshape(-1, 128).T)


def pack_consts(inp):
    cols = []
    offs = {}
    blocks = {}

    def put(name, a):
        a = np.asarray(a, np.float32)
        assert a.shape[0] == 128
        offs[name] = (sum(c.shape[1] for c in cols), a.shape[1])
        cols.append(a)

    def col(v):
        a = np.zeros((128, 1), np.float32)
        a[:len(v), 0] = v
        return a
    rep = lambda v: np.tile(np.asarray(v, np.float32)[None, :], (128, 1))
    cnt = [0, 0, 0]
    for l in range(4):
        start = sum(c.shape[1] for c in cols)
        put(f'norm_mix{l}', fm(inp['norm_mix'][l]))
        kind = l % 3
        i = cnt[kind]
        cnt[kind] += 1
        if kind == 0:
            for j in range(4):
                put(f'gdn_cw{i}_{j}', fm(inp['gdn_conv_w'][i][j]))
            put(f'gdn_alog_b{i}', rep(inp['gdn_a_log'][i]))
            put(f'gdn_dtb_b{i}', rep(inp['gdn_dt_bias'][i]))
            put(f'gdn_norm{i}', col(inp['gdn_norm'][i]))
        elif kind == 1:
            put(f'mla_nq{i}', fm(inp['mla_norm_q_lat'][i]))
            put(f'mla_nkv{i}', fm(inp['mla_norm_kv_lat'][i]))
            put(f'mla_qn_n{i}', col(inp['mla_q_norm'][i][:128]))
            put(f'mla_qn_r{i}', col(inp['mla_q_norm'][i][128:]))
            put(f'mla_kn_n{i}', col(inp['mla_k_norm'][i][:128]))
            put(f'mla_kn_r{i}', col(inp['mla_k_norm'][i][128:]))
        else:
            for j in range(4):
                put(f'ssm_cw{i}_{j}', fm(inp['ssm_conv_w'][i][j]))
            put(f'ssm_cb{i}', fm(inp['ssm_conv_b'][i]))
            put(f'ssm_alog_b{i}', rep(inp['ssm_a_log'][i]))
            put(f'ssm_dtb_b{i}', rep(inp['ssm_dt_bias'][i]))
            put(f'ssm_d_b{i}', rep(inp['ssm_d'][i]))
            put(f'ssm_norm{i}', fm(inp['ssm_norm'][i]))
        blocks[('mix', l)] = (start, sum(c.shape[1] for c in cols) - start)
        start = sum(c.shape[1] for c in cols)
        put(f'norm_mlp{l}', fm(inp['norm_mlp'][l]))
        blocks[('mlp', l)] = (start, 16)
    offs['__blocks__'] = blocks
    return np.concatenate(cols, axis=1), offs


def load_block(k, name):
    start, size = k.cvo['__blocks__'][name]
    k.cvbase = start
    k.p.add('sp', lambda e: e.dma_start(out=k.cv[:, :size], in_=k.dr['cvec'][:, start:start + size]), writes=['cv'], dma=True)


def build(plan, cvo, ncv, debug=False):
    nc = bass.Bass("TRN2", target_bir_lowering=False)
    dr = {}

    def din(name, shape):
        dr[name] = nc.dram_tensor(name, list(shape), F32, kind="ExternalInput").ap()

    din('x', [2048, D])
    din('meta', [NMETA, D])
    din('cvec', [128, ncv])
    din('ident', [128, 128])
    din('tri', [128, 128])
    din('rm', [64, 64])
    din('cosT', [64, L])
    din('sinT', [64, L])
    din('mla_w_in', [1, D, 1344])
    din('mla_w_uq', [1, 768, 3072])
    din('mla_w_ukv', [1, 512, 4096])
    din('mla_w_out', [1, D, D])
    din('gdn_w_in', [2, D, 12352])
    din('gdn_w_out', [2, 4096, D])
    din('ssm_w_in', [1, D, 10304])
    din('ssm_w_out', [1, 4096, D])
    dr['yTd'] = nc.dram_tensor('yTd', [32, 128, L], BF16).ap()
    din('mlp_w_up', [4, D, 4 * D])
    din('mlp_w_down', [4, 4 * D, D])
    dr['out'] = nc.dram_tensor('out', [2048, D], F32, kind="ExternalOutput").ap()
    dr['hT'] = nc.dram_tensor('hT', [16, 128, L], F32).ap()
    st = ExitStack()
    with st:
        sems = {e: [st.enter_context(nc.semaphore(f"s_{e}_{i}")) for i in range(N_EPOCH_SEMS)] for e in ENGS}
        sems['dma'] = [st.enter_context(nc.semaphore(f"s_dma_{i}")) for i in range(N_DMA_SEMS)]
        k = K(nc, st, dr)
        k.cvo, k.ncv = cvo, ncv
        k.debug = debug
        alloc(k)
        load_consts(k)
        load_input(k)
        for step in plan:
            k.p.barrier()
            if step[0] == 'mlp':
                load_block(k, ('mlp', step[1]))
                mlp(k, step[1])
            elif step[0] == 'mla':
                load_block(k, ('mix', step[2]))
                mla(k, step[1], step[2])
            elif step[0] == 'ssm':
                load_block(k, ('mix', step[2]))
                ssm(k, step[1], step[2])
            elif step[0] == 'gdn':
                load_block(k, ('mix', step[2]))
                gdn(k, step[1], step[2])
        k.p.barrier()
        store_output(k)
        run = k.p.emit_all(sems)
        block = st.enter_context(nc.Block())

        @block.tensor
        def _(e):
            run('pe', e)

        @block.vector
        def _(e):
            run('dve', e)

        @block.scalar
        def _(e):
            run('act', e)

        @block.gpsimd
        def _(e):
            run('pool', e)

        @block.sync
        def _(e):
            run('sp', e)
        print("prog stats", k.p.stats)
    return nc


FULL_PLAN = [('gdn', 0, 0), ('mlp', 0), ('mla', 0, 1), ('mlp', 1), ('ssm', 0, 2), ('mlp', 2), ('gdn', 1, 3), ('mlp', 3)]


def make_inmaps(inp, nb):
    cvec, cvo = pack_consts(inp)
    ident = np.eye(128, dtype=np.float32)
    tri = np.triu(np.ones((128, 128), np.float32))
    rm = np.zeros((64, 64), np.float32)
    for m_ in range(32):
        rm[m_ + 32, m_] = -1.0
        rm[m_, m_ + 32] = 1.0
    inv = 10000.0 ** (-np.arange(0, 64, 2, dtype=np.float32) / 64)
    ang = np.arange(L, dtype=np.float32)[:, None] * inv[None, :]
    ang = np.concatenate([ang, ang], axis=-1)
    cosT = np.ascontiguousarray(np.cos(ang).T.astype(np.float32))
    sinT = np.ascontiguousarray(np.sin(ang).T.astype(np.float32))
    maps = []
    for b in range(nb):
        m = {'x': np.ascontiguousarray(inp['x'][b]), 'meta': np.ascontiguousarray(inp['meta_tokens']), 'cvec': cvec, 'ident': ident, 'tri': tri, 'rm': rm, 'cosT': cosT, 'sinT': sinT,
             'gdn_w_in': np.asarray(inp['gdn_w_in']), 'gdn_w_out': np.asarray(inp['gdn_w_out']),
             'ssm_w_in': np.asarray(inp['ssm_w_in']), 'ssm_w_out': np.asarray(inp['ssm_w_out']),
             'mla_w_in': np.asarray(inp['mla_w_in']), 'mla_w_uq': np.asarray(inp['mla_w_uq']), 'mla_w_ukv': np.asarray(inp['mla_w_ukv']), 'mla_w_out': np.asarray(inp['mla_w_out']),
             'mlp_w_up': np.asarray(inp['mlp_w_up']), 'mlp_w_down': np.asarray(inp['mlp_w_down'])}
        maps.append(m)
    return maps, cvo, cvec.shape[1]


def kernel(**inp):
    inp = {k_: np.asarray(v) for k_, v in inp.items()}
    maps, cvo, ncv = make_inmaps(inp, 4)
    nc = build(FULL_PLAN, cvo, ncv)
    in_maps = [maps[c % 4] for c in range(8)]
    res = run_bass_kernel_spmd(nc, in_maps, core_ids=list(range(8)))
    return np.stack([res.results[b]['out'] for b in range(4)], axis=0).astype(np.float32)
```
